# Optimizing a Trainium2 kernel written in Bass

```python
import jax, jax.numpy as jnp
from jax import lax
import numpy as np

D_MODEL = 1024
BATCH = 4
SEQ = 8192
DEPTH = 2

HEAD_DIM = 64
N_MIXERS = 2
MOBA_HEADS = 16
MOBA_BLOCK = 256
MOBA_TOPK = 3
MOBA_Q_CHUNK = 32
DIL_GROUPS = ((128, 1), (512, 4), (2048, 16))
DIL_HEADS_PER_GROUP = 4
DIL_HEADS = DIL_HEADS_PER_GROUP * len(DIL_GROUPS)
DIL_BLOCK = 128
D_FF = 4 * D_MODEL
ROPE_THETA = 10000.0
LN_EPS = 1e-5
DEEPNORM_ALPHA = (2.0 * DEPTH) ** 0.25
DEEPNORM_BETA = (8.0 * DEPTH) ** -0.25
N_LAYERS_A = (DEPTH + N_MIXERS - 1) // N_MIXERS
N_LAYERS_B = DEPTH // N_MIXERS

kernel_name = "hybrid_moba_dilated_sqrelu_deepnorm"


def layer_norm(x, g, b):
    xf = x.astype(jnp.float32)
    mu = jnp.mean(xf, axis=-1, keepdims=True)
    var = jnp.mean(jnp.square(xf - mu), axis=-1, keepdims=True)
    y = (xf - mu) * lax.rsqrt(var + LN_EPS)
    return (y * g.astype(jnp.float32) + b.astype(jnp.float32)).astype(x.dtype)


def rotary_tables(seq, dtype):
    inv = 1.0 / (ROPE_THETA ** (jnp.arange(0, HEAD_DIM, 2, dtype=jnp.float32) / HEAD_DIM))
    ang = jnp.arange(seq, dtype=jnp.float32)[:, None] * inv[None, :]
    return jnp.cos(ang).astype(dtype), jnp.sin(ang).astype(dtype)


def apply_rotary(t, cos, sin):
    t1, t2 = jnp.split(t, 2, axis=-1)
    c = cos[None, :, None, :]
    s = sin[None, :, None, :]
    return jnp.concatenate([t1 * c - t2 * s, t2 * c + t1 * s], axis=-1)


def moba_attention(x, w_qkv, w_o, cos, sin):
    B, S, _ = x.shape
    H, dh, BLK, QC = MOBA_HEADS, HEAD_DIM, MOBA_BLOCK, MOBA_Q_CHUNK
    qkv = (x @ w_qkv).reshape(B, S, 3, H, dh)
    q = apply_rotary(qkv[:, :, 0], cos, sin) * (dh ** -0.5)
    k = apply_rotary(qkv[:, :, 1], cos, sin)
    v = qkv[:, :, 2]
    pad = (-S) % BLK
    Sp = S + pad
    nb = Sp // BLK
    padw = ((0, 0), (0, pad), (0, 0), (0, 0))
    q = jnp.pad(q, padw).transpose(0, 2, 1, 3)
    k = jnp.pad(k, padw).transpose(0, 2, 1, 3)
    v = jnp.pad(v, padw).transpose(0, 2, 1, 3)
    k_blk = k.reshape(B, H, nb, BLK, dh)
    v_blk = v.reshape(B, H, nb, BLK, dh)
    k_mean = jnp.mean(k_blk.astype(jnp.float32), axis=3).astype(k.dtype)

    gate = jnp.einsum('bhsd,bhnd->bhsn', q, k_mean).astype(jnp.float32)
    q_block = jnp.arange(Sp) // BLK
    past = jnp.arange(nb)[None, :] < q_block[:, None]
    gate = jnp.where(past[None, None], gate, -jnp.inf)
    kk = min(MOBA_TOPK, nb)
    sel_score, sel_idx = lax.top_k(gate, kk)
    sel_valid = jnp.isfinite(sel_score)

    nc = Sp // QC
    q_c = q.reshape(B, H, nc, QC, dh).transpose(2, 0, 1, 3, 4)
    idx_c = sel_idx.reshape(B, H, nc, QC, kk).transpose(2, 0, 1, 3, 4)
    val_c = sel_valid.reshape(B, H, nc, QC, kk).transpose(2, 0, 1, 3, 4)
    b_ix = jnp.arange(B)[:, None, None, None]
    h_ix = jnp.arange(H)[None, :, None, None]
    key_off = jnp.arange(BLK)

    def one_chunk(args):
        c, qc, idx, valid = args
        kg = k_blk[b_ix, h_ix, idx]
        vg = v_blk[b_ix, h_ix, idx]
        s_past = jnp.einsum('bhqd,bhqnkd->bhqnk', qc, kg).astype(jnp.float32)
        s_past = jnp.where(valid[..., None], s_past, -jnp.inf).reshape(B, H, QC, kk * BLK)
        start = c * QC
        blk = start // BLK
        k_own = lax.dynamic_index_in_dim(k_blk, blk, axis=2, keepdims=False)
        v_own = lax.dynamic_index_in_dim(v_blk, blk, axis=2, keepdims=False)
        s_own = jnp.einsum('bhqd,bhkd->bhqk', qc, k_own).astype(jnp.float32)
        q_off = start % BLK + jnp.arange(QC)
        causal = key_off[None, :] <= q_off[:, None]
        s_own = jnp.where(causal[None, None], s_own, -jnp.inf)
        p = jax.nn.softmax(jnp.concatenate([s_past, s_own], axis=-1), axis=-1).astype(v.dtype)
        p_past = p[..., :kk * BLK].reshape(B, H, QC, kk, BLK)
        p_own = p[..., kk * BLK:]
        return (jnp.einsum('bhqnk,bhqnkd->bhqd', p_past, vg)
                + jnp.einsum('bhqk,bhkd->bhqd', p_own, v_own))

    out = lax.map(one_chunk, (jnp.arange(nc, dtype=jnp.int32), q_c, idx_c, val_c))
    out = out.transpose(1, 0, 3, 2, 4).reshape(B, Sp, H * dh)[:, :S]
    return out @ w_o


def dilated_group(q, k, v, window, dilation):
    B, S, Hg, dh = q.shape
    span = window // dilation
    WB = DIL_BLOCK
    assert span <= WB
    L = S // dilation
    Lp = -(-L // WB) * WB
    nblk = Lp // WB

    def to_blocks(t):
        t = t.reshape(B, L, dilation, Hg, dh).transpose(0, 2, 1, 3, 4)
        t = jnp.pad(t, ((0, 0), (0, 0), (0, Lp - L), (0, 0), (0, 0)))
        return t.reshape(B, dilation, nblk, WB, Hg, dh)

    def with_prev(t):
        prev = jnp.pad(t, ((0, 0), (0, 0), (1, 0), (0, 0), (0, 0), (0, 0)))[:, :, :-1]
        return jnp.concatenate([prev, t], axis=3)

    qb = to_blocks(q)
    kw = with_prev(to_blocks(k))
    vw = with_prev(to_blocks(v))
    s = jnp.einsum('brnqhd,brnkhd->brnhqk', qb, kw).astype(jnp.float32)
    qi = jnp.arange(nblk)[:, None, None] * WB + jnp.arange(WB)[None, :, None]
    ki = jnp.arange(nblk)[:, None, None] * WB - WB + jnp.arange(2 * WB)[None, None, :]
    dist = qi - ki
    mask = (dist >= 0) & (dist <= span) & (ki >= 0)
    s = jnp.where(mask[None, None, :, None], s, -jnp.inf)
    lse = jax.nn.logsumexp(s, axis=-1)
    p = jnp.exp(s - lse[..., None]).astype(v.dtype)
    o = jnp.einsum('brnhqk,brnkhd->brnqhd', p, vw)
    o = o.reshape(B, dilation, Lp, Hg, dh)[:, :, :L].transpose(0, 2, 1, 3, 4).reshape(B, S, Hg, dh)
    lse = lse.transpose(0, 1, 2, 4, 3).reshape(B, dilation, Lp, Hg)[:, :, :L]
    lse = lse.transpose(0, 2, 1, 3).reshape(B, S, Hg)
    return o, lse


def dilated_attention(x, w_qkv, w_o, cos, sin):
    B, S, _ = x.shape
    G, Hg, dh = len(DIL_GROUPS), DIL_HEADS_PER_GROUP, HEAD_DIM
    qkv = (x @ w_qkv).reshape(B, S, 3, DIL_HEADS, dh)
    q = apply_rotary(qkv[:, :, 0], cos, sin) * (dh ** -0.5)
    k = apply_rotary(qkv[:, :, 1], cos, sin)
    v = qkv[:, :, 2]
    outs, lses = [], []
    for g, (window, dilation) in enumerate(DIL_GROUPS):
        hs = slice(g * Hg, (g + 1) * Hg)
        o_g, lse_g = dilated_group(q[:, :, hs], k[:, :, hs], v[:, :, hs], window, dilation)
        outs.append(o_g)
        lses.append(lse_g)
    alpha = jax.nn.softmax(jnp.stack(lses, axis=0), axis=0).astype(x.dtype)
    o = jnp.stack(outs, axis=0) * alpha[..., None]
    o = o.transpose(1, 2, 0, 3, 4).reshape(B, S, G * Hg * dh)
    return o @ w_o


def sq_relu_mlp(x, w_in, w_out):
    return jnp.square(jax.nn.relu(x @ w_in)) @ w_out


def setup_inputs(seed: int = 0) -> dict:
    key = jax.random.key(seed)
    ks = jax.random.split(key, 12)
    f32 = jnp.float32
    moba_w = MOBA_HEADS * HEAD_DIM
    dil_w = DIL_HEADS * HEAD_DIM
    x = jax.random.normal(ks[0], (BATCH, SEQ, D_MODEL), f32)
    moba_w_qkv = jax.random.normal(ks[1], (N_LAYERS_A, D_MODEL, 3 * moba_w), f32) * D_MODEL ** -0.5
    moba_w_o = jax.random.normal(ks[2], (N_LAYERS_A, moba_w, D_MODEL), f32) * (moba_w ** -0.5 * DEEPNORM_BETA)
    dil_w_qkv = jax.random.normal(ks[3], (N_LAYERS_B, D_MODEL, 3 * dil_w), f32) * D_MODEL ** -0.5
    dil_w_o = jax.random.normal(ks[4], (N_LAYERS_B, dil_w, D_MODEL), f32) * (dil_w ** -0.5 * DEEPNORM_BETA)
    mlp_w_in = jax.random.normal(ks[5], (DEPTH, D_MODEL, D_FF), f32) * D_MODEL ** -0.5
    mlp_w_out = jax.random.normal(ks[6], (DEPTH, D_FF, D_MODEL), f32) * (D_FF ** -0.5 * DEEPNORM_BETA)
    ln_mix_g = 1.0 + 0.02 * jax.random.normal(ks[7], (DEPTH, D_MODEL), f32)
    ln_mix_b = 0.02 * jax.random.normal(ks[8], (DEPTH, D_MODEL), f32)
    ln_mlp_g = 1.0 + 0.02 * jax.random.normal(ks[9], (DEPTH, D_MODEL), f32)
    ln_mlp_b = 0.02 * jax.random.normal(ks[10], (DEPTH, D_MODEL), f32)
    return {"x": x, "moba_w_qkv": moba_w_qkv, "moba_w_o": moba_w_o,
            "dil_w_qkv": dil_w_qkv, "dil_w_o": dil_w_o,
            "mlp_w_in": mlp_w_in, "mlp_w_out": mlp_w_out,
            "ln_mix_g": ln_mix_g, "ln_mix_b": ln_mix_b,
            "ln_mlp_g": ln_mlp_g, "ln_mlp_b": ln_mlp_b}


def reference(x, moba_w_qkv, moba_w_o, dil_w_qkv, dil_w_o, mlp_w_in, mlp_w_out,
              ln_mix_g, ln_mix_b, ln_mlp_g, ln_mlp_b):
    S = x.shape[1]
    cos, sin = rotary_tables(S, x.dtype)
    h = x
    for i in range(DEPTH):
        j = i // N_MIXERS
        if i % N_MIXERS == 0:
            mix = moba_attention(h, moba_w_qkv[j], moba_w_o[j], cos, sin)
        else:
            mix = dilated_attention(h, dil_w_qkv[j], dil_w_o[j], cos, sin)
        h = layer_norm(DEEPNORM_ALPHA * h + mix, ln_mix_g[i], ln_mix_b[i])
        h = layer_norm(DEEPNORM_ALPHA * h + sq_relu_mlp(h, mlp_w_in[i], mlp_w_out[i]),
                       ln_mlp_g[i], ln_mlp_b[i])
    return h
```

```python
import numpy as np
import ml_dtypes
import concourse.bass as bass
import concourse.mybir as mybir
from concourse.bass_utils import run_bass_kernel_spmd

F32 = mybir.dt.float32
BF16 = mybir.dt.bfloat16
AF = mybir.ActivationFunctionType
ALU = mybir.AluOpType
AX = mybir.AxisListType

D = 1024
S = 8192
B = 4
DFF = 4096
DH = 64
NT = S // 512
HALF = S // 2
LN_EPS = 1e-5
ALPHA = (2.0 * 2) ** 0.25
NEG = -30000.0
DIL = ((128, 1), (512, 4), (2048, 16))


class Res:
    __slots__ = ("w", "rc", "rd")

    def __init__(self):
        self.w = None
        self.rc = {}
        self.rd = []


class Op:
    __slots__ = ("eng", "fn", "deps", "sig", "val", "dma", "sem")


class Prog:
    ENGS = ("pe", "act", "dve", "pool", "sp")
    NDSEM = 12

    def __init__(self, nc):
        self.nc = nc
        self.ops = {e: [] for e in self.ENGS}
        self.dma_hist = {e: [None] * self.NDSEM for e in ("sp", "pool", "act")}
        self.dma_cnt = {e: 0 for e in ("sp", "pool", "act")}
        self.dma_val = {e: [0] * self.NDSEM for e in ("sp", "pool", "act")}

    def op(self, eng, fn, reads=(), writes=(), dma=False):
        o = Op()
        o.eng, o.fn, o.dma, o.sig, o.val, o.sem = eng, fn, dma, False, 0, None
        deps = {}
        for r in reads:
            if r.w is not None:
                deps[id(r.w)] = r.w
        for w in writes:
            if w.w is not None:
                deps[id(w.w)] = w.w
            for x in w.rc.values():
                deps[id(x)] = x
            for x in w.rd:
                deps[id(x)] = x
        if dma:
            k = self.dma_cnt[eng] % self.NDSEM
            self.dma_cnt[eng] += 1
            prev = self.dma_hist[eng][k]
            if prev is not None:
                deps[id(prev)] = prev
            self.dma_hist[eng][k] = o
            self.dma_val[eng][k] += 16
            o.sem = (eng, k)
            o.val = self.dma_val[eng][k]
        o.deps = []
        for d in deps.values():
            if d is o:
                continue
            if (not d.dma) and (not dma) and d.eng == "pe" and eng == "pe":
                continue
            if not d.dma:
                d.sig = True
            o.deps.append(d)
        for r in reads:
            if dma:
                r.rd.append(o)
            else:
                r.rc[eng] = o
        for w in writes:
            w.w = o
            w.rc = {}
            w.rd = []
        self.ops[eng].append(o)
        return o

    def emit(self, pool=None):
        nc = self.nc
        if pool is None:
            pool = SemPool(nc)
        for e in self.ENGS:
            c = pool.ebase[e]
            for o in self.ops[e]:
                if o.sig and not o.dma:
                    c += 1
                    o.val = c
            pool.ebase[e] = c
        for e in self.ENGS:
            for o in self.ops[e]:
                if o.dma:
                    o.val += pool.dbase[o.sem]
        esem, dsem = pool.esem, pool.dsem
        with nc.Block() as block:
            def run(e, eng):
                waited = {}
                for o in self.ops[e]:
                    for d in o.deps:
                        s = dsem[d.sem] if d.dma else esem[d.eng]
                        key = d.sem if d.dma else d.eng
                        if waited.get(key, 0) < d.val:
                            eng.wait_ge(s, d.val)
                            waited[key] = d.val
                    ins = o.fn(eng)
                    if o.dma:
                        ins.then_inc(dsem[o.sem], 16)
                    elif o.sig:
                        ins.then_inc(esem[e], 1)
                if e in ("sp", "pool") and self.dma_cnt[e]:
                    for k in range(min(self.NDSEM, self.dma_cnt[e])):
                        eng.wait_ge(dsem[(e, k)], pool.dbase[(e, k)] + self.dma_val[e][k])

            @block.tensor
            def _(t):
                run("pe", t)

            @block.scalar
            def _(s):
                run("act", s)

            @block.vector
            def _(v):
                run("dve", v)

            @block.gpsimd
            def _(g):
                run("pool", g)

            @block.sync
            def _(sy):
                run("sp", sy)
        for e in ("sp", "pool"):
            for k in range(self.NDSEM):
                pool.dbase[(e, k)] += self.dma_val[e][k]


class SemPool:
    def __init__(self, nc):
        _UID[0] += 1
        u = "_%d" % _UID[0]
        self.nc = nc
        self.esem = {e: nc.alloc_semaphore("es_" + e + u) for e in Prog.ENGS}
        self.dsem = {(q, k): nc.alloc_semaphore("ds_%s%d%s" % (q, k, u)) for q in ("sp", "pool") for k in range(Prog.NDSEM)}
        self.ebase = {e: 0 for e in Prog.ENGS}
        self.dbase = {key: 0 for key in self.dsem}


_UID = [0]


class Tile:
    def __init__(self, t, nres=1):
        self.t = t
        self.r = [Res() for _ in range(nres)]

    def __getitem__(self, k):
        return self.t[k]


class Ctx:
    def __init__(self, nc):
        import contextlib
        self.nc = nc
        self.P = Prog(nc)
        self.st = contextlib.ExitStack()
        self.n = 0

    def sb(self, shape, dt, nres=1):
        _UID[0] += 1
        return Tile(self.st.enter_context(self.nc.sbuf_tensor("sb%d" % _UID[0], list(shape), dt)), nres)

    def ps(self, shape, dt=F32, nres=1):
        _UID[0] += 1
        return Tile(self.st.enter_context(self.nc.psum_tensor("ps%d" % _UID[0], list(shape), dt)), nres)

    def dram(self, name, shape, dt, kind="Internal"):
        return self.nc.dram_tensor(name, list(shape), dt, kind=kind)


class Rot:
    def __init__(self, tiles):
        self.tiles = tiles
        self.i = 0

    def next(self):
        t = self.tiles[self.i % len(self.tiles)]
        self.i += 1
        return t


def dma(P, q, out, in_, reads, writes):
    def f(e):
        return e.dma_start(out=out(e) if callable(out) else out, in_=in_(e) if callable(in_) else in_)
    return P.op(q, f, reads=reads, writes=writes, dma=True)


def phase_a(cx, hsrc, h_is_f32, wqkv, cos_d, sin_d, rotm_d, qT, kT, vv, nh, hres, ores, hook=None):
    nc, P = cx.nc, cx.P
    npair = nh // 2
    ncol = nh * DH
    w_sb = cx.sb([128, 8, 3 * ncol], BF16, nres=8)
    rot_sb = cx.sb([128, 128], BF16)
    wv = wqkv.ap().rearrange("(c p) n -> p c n", p=128)
    for c in range(8):
        dma(P, "pool", w_sb[:, c, :], wv[:, c, :], [], [w_sb.r[c]])
    dma(P, "pool", rot_sb[:, :], rotm_d.ap(), [], [rot_sb.r[0]])
    hts = Rot([cx.sb([128, 8, 512], BF16) for _ in range(2)])
    cs = Rot([cx.sb([128, 2, 512], F32, nres=2) for _ in range(2)])
    tbs = Rot([cx.sb([128, 512], BF16) for _ in range(3)])
    tmp1 = Rot([cx.sb([128, 512], F32) for _ in range(2)])
    tmp2 = Rot([cx.sb([128, 512], F32) for _ in range(2)])
    outs = Rot([cx.sb([128, 512], BF16) for _ in range(3)])
    vouts = Rot([cx.sb([128, ncol], BF16) for _ in range(2)])
    psA = Rot([cx.ps([128, 512]) for _ in range(3)])
    psR = Rot([cx.ps([128, 512]) for _ in range(2)])
    psV = Rot([cx.ps([128, ncol]) for _ in range(2)])
    cosv = cos_d.ap()
    sinv = sin_d.ap()
    def load_h(t):
        ht = hts.next()
        for (c0, c1, hap) in hsrc(t):
            dma(P, "pool", ht[:, c0:c1, :], hap, [hres], [ht.r[0]])
        ct = cs.next()
        dma(P, "pool", ct[:, 0, :], cosv[:, t * 512:(t + 1) * 512], [], [ct.r[0]])
        dma(P, "pool", ct[:, 1, :], sinv[:, t * 512:(t + 1) * 512], [], [ct.r[1]])
        return ht, ct
    nxt_h = load_h(0)
    for t in range(NT):
        ht, ct = nxt_h
        if t + 1 < NT:
            nxt_h = load_h(t + 1)
        if hook is not None:
            hook(t)
        pend = None
        jobs = [(which, p) for which in range(2) for p in range(npair)]
        for (which, p) in jobs + [(None, None)]:
            cur = None
            if (which, p) is not None and p is not None:
                pass
            if which is not None:
                dst = qT if which == 0 else kT
                col0 = which * ncol + p * 128
                pa = psA.next()

                def mm(e, pa=pa, ht=ht, col0=col0):
                    for c in range(8):
                        ins = e.matmul(pa[:, :], w_sb[:, c, col0:col0 + 128], ht[:, c, :], start=(c == 0), stop=(c == 7))
                    return ins
                P.op("pe", mm, list(w_sb.r) + [ht.r[0]], [pa.r[0]])
                tb = tbs.next()
                P.op("act", lambda e, tb=tb, pa=pa: e.activation(out=tb[:, :], in_=pa[:, :], func=AF.Copy),
                     [], [tb.r[0], pa.r[0]])
                cur = (which, p, dst, pa, tb)
            if pend is not None:
                pw, pp, pdst, ppa, ptb = pend
                pr = psR.next()
                P.op("pe", lambda e, pr=pr, tb=ptb: e.matmul(pr[:, :], rot_sb[:, :], tb[:, :], start=True, stop=True),
                     [rot_sb.r[0], ptb.r[0]], [pr.r[0]])
                t1 = tmp1.next()
                t2 = tmp2.next()
                sc = 0.125 if pw == 0 else 1.0
                P.op("dve", lambda e, t1=t1, pa=ppa, ct=ct, sc=sc: e.scalar_tensor_tensor(
                    out=t1[:, :], in0=pa[:, :], scalar=sc, in1=ct[:, 0, :], op0=ALU.mult, op1=ALU.mult),
                    [ct.r[0]], [t1.r[0], ppa.r[0]])
                P.op("dve", lambda e, t2=t2, pr=pr, ct=ct, sc=sc: e.scalar_tensor_tensor(
                    out=t2[:, :], in0=pr[:, :], scalar=sc, in1=ct[:, 1, :], op0=ALU.mult, op1=ALU.mult),
                    [ct.r[1]], [t2.r[0], pr.r[0]])
                ob = outs.next()
                P.op("pool", lambda e, ob=ob, t1=t1, t2=t2: e.tensor_tensor(out=ob[:, :], in0=t1[:, :], in1=t2[:, :], op=ALU.add),
                     [t1.r[0], t2.r[0]], [ob.r[0]])
                dma(P, "sp", pdst.ap()[pp * 128:(pp + 1) * 128, t * 512:(t + 1) * 512], ob[:, :], [ob.r[0]], [ores])
            pend = cur
        for s4 in range(4):
            pv = psV.next()

            def mmv(e, pv=pv, ht=ht, s4=s4):
                for c in range(8):
                    ins = e.matmul(pv[:, :], ht[:, c, s4 * 128:(s4 + 1) * 128], w_sb[:, c, 2 * ncol:3 * ncol],
                                   start=(c == 0), stop=(c == 7))
                return ins
            P.op("pe", mmv, list(w_sb.r) + [ht.r[0]], [pv.r[0]])
            vo = vouts.next()
            P.op("act", lambda e, vo=vo, pv=pv: e.activation(out=vo[:, :], in_=pv[:, :], func=AF.Copy),
                 [], [vo.r[0], pv.r[0]])
            r0 = t * 512 + s4 * 128
            dma(P, "sp", vv.ap()[r0:r0 + 128, :], vo[:, :], [vo.r[0]], [ores])


def rope_tables():
    inv = (1.0 / (10000.0 ** (np.arange(0, DH, 2, dtype=np.float32) / DH))).astype(np.float32)
    ang = (np.arange(S, dtype=np.float32)[:, None] * inv[None, :]).astype(np.float32)
    cos = np.cos(ang).astype(np.float32).T
    sin = np.sin(ang).astype(np.float32).T
    cos128 = np.ascontiguousarray(np.tile(cos, (4, 1)))
    sin128 = np.ascontiguousarray(np.tile(sin, (4, 1)))
    R = np.zeros((128, 128), np.float32)
    for hh in range(2):
        for j in range(32):
            R[hh * 64 + 32 + j, hh * 64 + j] = -1.0
            R[hh * 64 + j, hh * 64 + 32 + j] = 1.0
    return cos128, sin128, R


def precast_weights(cx, wo_d, win_d, wout_d, wp, wres, nk=8, stg=None):
    P = cx.P
    if stg is None:
        stg = Rot([cx.sb([128, 8192], BF16, nres=4) for _ in range(2)])
    wov = wo_d.ap().rearrange("(k p) d -> p k d", p=128)
    winv = win_d.ap().rearrange("(k p) f -> p k f", p=128)
    woutv = wout_d.ap().rearrange("(f p) d -> p f d", p=128)

    def piece(i):
        st = stg.next()
        if i == 0:
            src, view = wov, st.t[:, 0:nk * 1024].rearrange("p (k d) -> p k d", k=nk)
        elif i <= 4:
            j = i - 1
            src, view = winv[:, :, j * 1024:(j + 1) * 1024], st.t[:, :].rearrange("p (k f) -> p k f", k=8)
        else:
            j = i - 5
            src, view = woutv[:, :, j * 256:(j + 1) * 256], st.t[:, :].rearrange("p (f d) -> p f d", f=32)
        nsp = 2
        n0 = view.shape[1] // nsp
        for q in range(nsp):
            dma(P, "pool", view[:, q * n0:(q + 1) * n0, :], src[:, q * n0:(q + 1) * n0, :], [], [st.r[q]])
        dma(P, "sp", wp.ap()[i], st[:, :], list(st.r), [wres])
    return piece


def phase_c(cx, attn_src, res_src, wp, lnp_d, out_f32_dst, out_bf16_dst, ares, rres, wres, ores, nk=8, hsel_d=None):
    P = cx.P
    ones = cx.sb([128, 128], BF16)
    P.op("pool", lambda e: e.memset(ones[:, :], 1.0), [], [ones.r[0]])
    lnp = cx.sb([128, 32], F32)
    dma(P, "sp", lnp[:, :], lnp_d.ap(), [], [lnp.r[0]])
    hsel = cx.sb([128, 2], F32)
    if hsel_d is not None:
        dma(P, "sp", hsel[:, :], hsel_d.ap(), [], [hsel.r[0]])
    wsl = Rot([cx.sb([128, 8192], BF16) for _ in range(4)])
    ats = Rot([cx.sb([128, 8, 512], BF16, nres=8) for _ in range(1)])
    ob16 = cx.sb([128, 8, 512], BF16, nres=8)
    Rs = Rot([cx.sb([128, 8, 512], F32, nres=8) for _ in range(2)])
    hb = cx.sb([128, 8, 512], BF16, nres=8)
    u = cx.sb([128, 32, 512], BF16, nres=32)
    zb = cx.sb([128, 8, 512], BF16, nres=8)
    zq = cx.sb([128, 8, 512], BF16, nres=8)
    mt = cx.sb([128, 512], F32)
    msq = cx.sb([128, 512], F32)
    rstd = cx.sb([128, 512], F32)
    rl = Rot([cx.sb([128, 512], F32) for _ in range(2)])
    acc = Rot([cx.ps([128, 512]) for _ in range(4)])
    st1 = cx.ps([128, 512])
    st2 = cx.ps([128, 512])
    wpv = wp.ap()

    def load_piece(i):
        w = wsl.next()
        for q in range(2):
            dma(P, "pool", w[:, q * 4096:(q + 1) * 4096], wpv[i][:, q * 4096:(q + 1) * 4096], [wres], [w.r[0]])
        return w

    def ln_pre(R):
        allr = list(R.r)
        P.op("act", lambda e: e.activation(out=zb[:, :, :], in_=R[:, :, :], func=AF.Copy), allr, list(zb.r))
        P.op("act", lambda e: e.activation(out=zq[:, :, :], in_=R[:, :, :], func=AF.Square), allr, list(zq.r))

    def ln_stats():
        def s1(e):
            for c in range(8):
                ins = e.matmul(st1[:, :], ones[:, :], zb[:, c, :], start=(c == 0), stop=(c == 7))
            return ins

        def s2(e):
            for c in range(8):
                ins = e.matmul(st2[:, :], ones[:, :], zq[:, c, :], start=(c == 0), stop=(c == 7))
            return ins
        P.op("pe", s1, [ones.r[0]] + list(zb.r), [st1.r[0]])
        P.op("pe", s2, [ones.r[0]] + list(zq.r), [st2.r[0]])

    def ln_post(R, gi, bi, bft, f32_dst, bf_dst):
        allr = list(R.r)
        P.op("act", lambda e: e.activation(out=mt[:, :], in_=st1[:, :], func=AF.Copy, scale=1.0 / D), [], [mt.r[0], st1.r[0]])
        P.op("dve", lambda e: e.tensor_tensor(out=msq[:, :], in0=mt[:, :], in1=mt[:, :], op=ALU.mult), [mt.r[0]], [msq.r[0]])
        P.op("dve", lambda e: e.scalar_tensor_tensor(out=rstd[:, :], in0=st2[:, :], scalar=1.0 / D, in1=msq[:, :],
                                                      op0=ALU.mult, op1=ALU.subtract), [msq.r[0]], [rstd.r[0], st2.r[0]])
        P.op("dve", lambda e: e.tensor_scalar_add(out=rstd[:, :], in0=rstd[:, :], scalar1=LN_EPS), [], [rstd.r[0]])
        P.op("act", lambda e: e.activation(out=rstd[:, :], in_=rstd[:, :], func=AF.Sqrt), [], [rstd.r[0]])
        P.op("dve", lambda e: e.reciprocal(out=rstd[:, :], in_=rstd[:, :]), [], [rstd.r[0]])
        for c in range(8):
            P.op("dve", lambda e, c=c: e.tensor_tensor(out=R[:, c, :], in0=R[:, c, :], in1=mt[:, :], op=ALU.subtract),
                 [mt.r[0]], [R.r[c]])
            P.op("dve", lambda e, c=c: e.tensor_tensor(out=R[:, c, :], in0=R[:, c, :], in1=rstd[:, :], op=ALU.mult),
                 [rstd.r[0]], [R.r[c]])
            if bft is not None:
                P.op("act", lambda e, c=c: e.activation(out=bft[:, c, :], in_=R[:, c, :], func=AF.Identity,
                                                         scale=lnp[:, gi * 8 + c:gi * 8 + c + 1], bias=lnp[:, bi * 8 + c:bi * 8 + c + 1]),
                     [lnp.r[0], R.r[c]], [bft.r[c]])
            P.op("act", lambda e, c=c: e.activation(out=R[:, c, :], in_=R[:, c, :], func=AF.Identity,
                                                     scale=lnp[:, gi * 8 + c:gi * 8 + c + 1], bias=lnp[:, bi * 8 + c:bi * 8 + c + 1]),
                 [lnp.r[0]], [R.r[c]])
        if f32_dst is not None:
            dma(P, "sp", f32_dst, R[:, :, :], allr, [ores])
        if bf_dst is not None:
            for (c0, c1, dap) in bf_dst:
                dma(P, "sp", dap, bft[:, c0:c1, :], list(bft.r[c0:c1]), [ores])

    ntile = HALF // 512
    Rt = {}

    def emit_wo(tt):
        at = ats.next()
        R = Rs.next()
        Rt[tt] = R
        if hsel_d is None:
            dma(P, "sp", at[:, 0:nk, :], attn_src(tt), [ares], list(at.r))
        else:
            s0, s1 = attn_src(tt)
            for k in range(nk):
                dma(P, "sp", zb[:, k, :], s0[k], [ares], [zb.r[k]])
                dma(P, "sp", zq[:, k, :], s1[k], [ares], [zq.r[k]])
            P.op("act", lambda e: e.activation(out=zq[:, 0:nk, :], in_=zq[:, 0:nk, :], func=AF.Identity, scale=hsel[:, 1:2]), [hsel.r[0]], list(zq.r))
            P.op("dve", lambda e, at=at: e.scalar_tensor_tensor(out=at[:, 0:nk, :], in0=zb[:, 0:nk, :], scalar=hsel[:, 0:1], in1=zq[:, 0:nk, :],
                                                             op0=ALU.mult, op1=ALU.add), [hsel.r[0]] + list(zb.r) + list(zq.r), list(at.r))
        dma(P, "sp", R[:, :, :], res_src(tt), [rres], list(R.r))
        w = load_piece(0)
        wv = w.t[:, 0:nk * 1024].rearrange("p (k d) -> p k d", k=nk)
        for c in range(8):
            pa = acc.next()

            def mmo(e, pa=pa, c=c, wv=wv, at=at):
                for k in range(nk):
                    ins = e.matmul(pa[:, :], wv[:, k, c * 128:(c + 1) * 128], at[:, k, :], start=(k == 0), stop=(k == nk - 1))
                return ins
            P.op("pe", mmo, [w.r[0]] + list(at.r), [pa.r[0]])
            P.op("dve", lambda e, pa=pa, c=c, R=R: e.scalar_tensor_tensor(out=R[:, c, :], in0=R[:, c, :], scalar=ALPHA, in1=pa[:, :],
                                                                      op0=ALU.mult, op1=ALU.add), [], [R.r[c], pa.r[0]])
        ln_pre(R)

    def emit_up(tt, j):
        w = load_piece(1 + j)
        wv = w.t[:, :].rearrange("p (k f) -> p k f", k=8)
        for f in range(8):
            pa = acc.next()

            def mmu(e, pa=pa, f=f, wv=wv):
                for k in range(8):
                    ins = e.matmul(pa[:, :], wv[:, k, f * 128:(f + 1) * 128], hb[:, k, :], start=(k == 0), stop=(k == 7))
                return ins
            P.op("pe", mmu, [w.r[0]] + list(hb.r), [pa.r[0]])
            r_ = rl.next()
            P.op("act", lambda e, pa=pa, r_=r_: e.activation(out=r_[:, :], in_=pa[:, :], func=AF.Relu), [], [r_.r[0], pa.r[0]])
            fi = j * 8 + f
            P.op("dve", lambda e, pa=pa, r_=r_, fi=fi: e.tensor_tensor(out=u[:, fi, :], in0=r_[:, :], in1=pa[:, :], op=ALU.mult),
                 [r_.r[0]], [u.r[fi], pa.r[0]])

    def emit_down(tt, j):
        R = Rt[tt]
        w = load_piece(5 + j)
        wv = w.t[:, :].rearrange("p (f d) -> p f d", f=32)
        for cc in range(2):
            c = 2 * j + cc
            pa = acc.next()

            def mmd(e, pa=pa, cc=cc, wv=wv):
                for f in range(32):
                    ins = e.matmul(pa[:, :], wv[:, f, cc * 128:(cc + 1) * 128], u[:, f, :], start=(f == 0), stop=(f == 31))
                return ins
            P.op("pe", mmd, [w.r[0]] + list(u.r), [pa.r[0]])
            P.op("dve", lambda e, pa=pa, c=c, R=R: e.scalar_tensor_tensor(out=R[:, c, :], in0=R[:, c, :], scalar=ALPHA, in1=pa[:, :],
                                                                      op0=ALU.mult, op1=ALU.add), [], [R.r[c], pa.r[0]])

    def ln1_finish(tt):
        ln_stats()
        ln_post(Rt[tt], 0, 1, hb, None, None)

    def ln2_stats_post(tt):
        ln_stats()
        bf = out_bf16_dst(tt) if out_bf16_dst is not None else None
        ln_post(Rt[tt], 2, 3, ob16 if bf is not None else None, out_f32_dst(tt), bf)

    emit_wo(0)
    ln1_finish(0)
    for j in range(4):
        emit_up(0, j)
    for tt in range(ntile):
        nxt = tt + 1 < ntile
        if nxt:
            emit_wo(tt + 1)
        emit_down(tt, 0)
        emit_down(tt, 1)
        if nxt:
            ln1_finish(tt + 1)
        emit_down(tt, 2)
        emit_down(tt, 3)
        ln_pre(Rt[tt])
        if nxt:
            emit_up(tt + 1, 0)
        ln2_stats_post(tt)
        if nxt:
            for j in range(1, 4):
                emit_up(tt + 1, j)


def phase_b_moba(cx, qT, kT, vv, onehot_d, tri_d, ident_d, gconst_d, attn_dst, nh, ires, ores, hook=None):
    P = cx.P
    NQT = S // 128
    Qa = Rot([cx.sb([96, S], BF16, nres=2) for _ in range(2)])
    Ka = Rot([cx.sb([96, S], BF16, nres=2) for _ in range(2)])
    Vh = Rot([cx.sb([128, 64, 128], BF16, nres=2) for _ in range(2)])
    tri = cx.sb([128, 4, 512], BF16)
    ident = cx.sb([128, 128], BF16)
    gcon = cx.sb([128, 3, 2048], F32)
    kmf = cx.sb([64, 32], F32)
    kmb = cx.sb([96, 32], BF16, nres=2)
    gs = cx.sb([128, 512], F32)
    m8 = cx.sb([128, 16, 8], F32)
    sel = cx.sb([128, 512], F32)
    bpad = cx.sb([128, 16, 96], BF16)
    pts = Rot([cx.sb([128, 2, 512], BF16) for _ in range(4)])
    rcp = Rot([cx.sb([128, 512], F32) for _ in range(2)])
    aos = Rot([cx.sb([64, 512], BF16) for _ in range(2)])
    sps = Rot([cx.ps([128, 1024]) for _ in range(3)])
    ops_ = Rot([cx.ps([128, 512]) for _ in range(2)])
    dma(P, "sp", tri[:, :, :], tri_d.ap(), [], [tri.r[0]])
    dma(P, "sp", ident[:, :], ident_d.ap(), [], [ident.r[0]])
    dma(P, "sp", gcon[:, :, :], gconst_d.ap(), [], [gcon.r[0]])
    for b_ in Ka.tiles:
        dma(P, "sp", b_[64:96, :], onehot_d.ap(), [], [b_.r[1]])
    vview = vv.ap().rearrange("(kt p) n -> p kt n", p=128)
    def load_head(h):
        qa, ka, vh = Qa.next(), Ka.next(), Vh.next()
        for q4 in range(4):
            sl = slice(q4 * 2048, (q4 + 1) * 2048)
            dma(P, "pool", qa[0:64, sl], qT.ap()[h * 64:(h + 1) * 64, sl], [ires], [qa.r[0]])
            dma(P, "pool", ka[0:64, sl], kT.ap()[h * 64:(h + 1) * 64, sl], [ires], [ka.r[0]])
        for q4 in range(4):
            dma(P, "pool", vh[:, q4 * 16:(q4 + 1) * 16, 0:64], vview[:, q4 * 16:(q4 + 1) * 16, h * 64:(h + 1) * 64], [ires], [vh.r[0]])
        return qa, ka, vh
    nxt_head = load_head(0)
    for b_ in Vh.tiles:
        P.op("pool", lambda e, b_=b_: e.memset(b_[:, :, 64:128], 1.0), [], [b_.r[1]])
    P.op("pool", lambda e: e.memset(bpad[:, :, :], 0.0), [], [bpad.r[0]])
    P.op("pool", lambda e: e.memset(kmb[64:96, :], 0.0), [], [kmb.r[1]])
    for b_ in Qa.tiles:
        P.op("pool", lambda e, b_=b_: e.memset(b_[64:96, :], 0.0), [], [b_.r[1]])
    for h in range(nh):
        qa, ka, vh = nxt_head
        if h + 1 < nh:
            nxt_head = load_head(h + 1)
        if hook is not None:
            hook(h)
        def make_prep(qa, ka):
            pieces = []

            def p_km():
                P.op("dve", lambda e, ka=ka: e.tensor_reduce(out=kmf[:, :], in_=ka[0:64, :].rearrange("p (n k) -> p n k", k=256),
                                                              axis=AX.X, op=ALU.add), [ka.r[0]], [kmf.r[0]])
                P.op("dve", lambda e: e.tensor_scalar_mul(out=kmb[0:64, :], in0=kmf[:, :], scalar1=1.0 / 256), [kmf.r[0]], [kmb.r[0]])
            pieces.append(p_km)

            def p_gate(g):
                gp = sps.next()

                def gmm(e, gp=gp, g=g, qa=qa):
                    for j in range(16):
                        qt = g * 16 + j
                        ins = e.matmul(gp[:, j * 32:(j + 1) * 32], qa[0:96, qt * 128:(qt + 1) * 128], kmb[:, :], start=True, stop=True)
                    return ins
                P.op("pe", gmm, [qa.r[0], qa.r[1], kmb.r[0], kmb.r[1]], [gp.r[0]])
                cs_ = slice(g * 512, (g + 1) * 512)
                P.op("dve", lambda e, gp=gp, cs_=cs_: e.tensor_tensor(out=gs[:, :], in0=gp[:, 0:512], in1=gcon[:, 0, cs_], op=ALU.add),
                     [gcon.r[0]], [gs.r[0], gp.r[0]])

                def mx(e):
                    for j in range(16):
                        ins = e.max(out=m8[:, j, :], in_=gs[:, j * 32:(j + 1) * 32])
                    return ins
                P.op("dve", mx, [gs.r[0]], [m8.r[0]])

                def ge(e):
                    for j in range(16):
                        ins = e.tensor_scalar(out=sel[:, j * 32:(j + 1) * 32], in0=gs[:, j * 32:(j + 1) * 32],
                                              scalar1=m8[:, j, 2:3], scalar2=None, op0=ALU.is_ge)
                    return ins
                P.op("dve", ge, [gs.r[0], m8.r[0]], [sel.r[0]])
                P.op("dve", lambda e, cs_=cs_: e.tensor_tensor(out=sel[:, :], in0=sel[:, :], in1=gcon[:, 1, cs_], op=ALU.mult),
                     [gcon.r[0]], [sel.r[0]])
                P.op("dve", lambda e, cs_=cs_: e.tensor_tensor(out=sel[:, :], in0=sel[:, :], in1=gcon[:, 2, cs_], op=ALU.add),
                     [gcon.r[0]], [sel.r[0]])
                P.op("dve", lambda e: e.tensor_scalar(out=bpad[:, :, 64:96], in0=sel[:, :].rearrange("p (j n) -> p j n", n=32),
                                                       scalar1=1.0, scalar2=-NEG, op0=ALU.subtract, op1=ALU.mult),
                     [sel.r[0]], [bpad.r[0]])
                for j4 in range(4):
                    tp = sps.next()

                    def tmm(e, tp=tp, j4=j4):
                        for jj in range(4):
                            ins = e.matmul(tp[0:96, jj * 128:(jj + 1) * 128], bpad[:, j4 * 4 + jj, :], ident[:, :], start=True, stop=True)
                        return ins
                    P.op("pe", tmm, [bpad.r[0], ident.r[0]], [tp.r[0]])
                    q0 = (g * 16 + j4 * 4) * 128
                    P.op("act", lambda e, tp=tp, q0=q0, qa=qa: e.activation(out=qa[64:96, q0:q0 + 512], in_=tp[64:96, 0:512], func=AF.Copy),
                         [], [qa.r[1], tp.r[0]])
            for g in range(4):
                pieces.append(lambda g=g: p_gate(g))
            return pieces

        if h == 0:
            for pc_ in make_prep(qa, ka):
                pc_()
        nxt_prep = make_prep(nxt_head[0], nxt_head[1]) if h + 1 < nh else []
        ins_at = {3: 0, 6: 1, 9: 2, 12: 3, 14: 4}
        groups = [(I, kt) for I in range(NT) for kt in range(0, 4 * (I + 1), 2)]
        opt = {}

        def emit_qk(gi, qa=qa, ka=ka):
            I, kt = groups[gi]
            qsl = slice(I * 512, (I + 1) * 512)
            sp_ = sps.next()

            def smm(e, sp_=sp_, kt=kt, qsl=qsl):
                for j in range(2):
                    ins = e.matmul(sp_[:, j * 512:(j + 1) * 512], ka[0:96, (kt + j) * 128:(kt + j + 1) * 128], qa[0:96, qsl],
                                   start=True, stop=True)
                return ins
            P.op("pe", smm, [qa.r[0], qa.r[1], ka.r[0], ka.r[1]], [sp_.r[0]])
            pt = pts.next()
            P.op("act", lambda e, pt=pt, sp_=sp_: e.activation(out=pt[:, :, :].rearrange("p a b -> p (a b)"), in_=sp_[:, :], func=AF.Exp),
                 [], [pt.r[0], sp_.r[0]])
            if kt >= 4 * I:
                d0 = kt - 4 * I
                P.op("dve", lambda e, pt=pt, d0=d0: e.tensor_tensor(out=pt[:, :, :], in0=pt[:, :, :], in1=tri[:, d0:d0 + 2, :], op=ALU.mult),
                     [tri.r[0]], [pt.r[0]])
            return pt

        def emit_pv(gi, pt, vh=vh, h=h):
            I, kt = groups[gi]
            nkt = 4 * (I + 1)
            if kt == 0:
                opt[I] = ops_.next()
            op_ = opt[I]

            def pv(e, op_=op_, pt=pt, kt=kt, nkt=nkt):
                for j in range(2):
                    ins = e.matmul(op_[:, :], vh[:, kt + j, :], pt[:, j, :], start=(kt + j == 0), stop=(kt + j == nkt - 1))
                return ins
            P.op("pe", pv, [vh.r[0], vh.r[1], pt.r[0]], [op_.r[0]])
            if kt + 2 == nkt:
                rc = rcp.next()
                ao = aos.next()
                P.op("dve", lambda e, rc=rc, op_=op_: e.reciprocal(out=rc[64:128, :], in_=op_[64:128, :]), [], [rc.r[0], op_.r[0]])
                P.op("dve", lambda e, rc=rc, op_=op_, ao=ao: e.tensor_tensor(out=ao[:, :], in0=op_[0:64, :], in1=rc[64:128, :], op=ALU.mult),
                     [rc.r[0]], [ao.r[0], op_.r[0]])
                dma(P, "sp", attn_dst(h, I), ao[:, :], [ao.r[0]], [ores])

        LA = 2
        pend = [emit_qk(gi) for gi in range(min(LA, len(groups)))]
        for gi in range(len(groups)):
            I_, kt_ = groups[gi]
            if kt_ == 0 and I_ in ins_at and nxt_prep:
                nxt_prep[ins_at[I_]]()
            if gi + LA < len(groups):
                pend.append(emit_qk(gi + LA))
            emit_pv(gi, pend.pop(0))


def moba_consts():
    onehot = np.zeros((32, S), np.float32)
    for n in range(32):
        onehot[n, n * 256:(n + 1) * 256] = 1.0
    k = np.arange(128)[:, None]
    q = np.arange(512)[None, :]
    tri = np.stack([((j * 128 + k) <= q).astype(np.float32) for j in range(4)], axis=1)
    ident = np.eye(128, dtype=np.float32)
    qb = (np.arange(64) // 2)[:, None]
    n = np.arange(32)[None, :]
    valid = (n < qb).astype(np.float32)
    own = (n == qb).astype(np.float32)
    cmask = (valid - 1.0) * 1e30
    g = np.stack([cmask.reshape(-1), valid.reshape(-1), own.reshape(-1)], axis=0).astype(np.float32)
    gconst = np.ascontiguousarray(np.broadcast_to(g[None], (128, 3, 2048))).astype(np.float32)
    return onehot, np.ascontiguousarray(tri), ident, gconst


DIL_D = (1, 4, 16)
DIL_MOFF = (0, 5, 13)
DIL_NM = 33


def phase_b_dil(cx, qT, kT, vv, dmask_d, attn_dst, ires, ores):
    P = cx.P
    Ks = [cx.sb([128, S], BF16, nres=2) for _ in range(3)]
    Vs = [cx.sb([128, 64, 128], BF16, nres=2) for _ in range(3)]
    mk = cx.sb([128, DIL_NM, 512], BF16, nres=DIL_NM)
    qts = Rot([cx.sb([128, 512], BF16, nres=2) for _ in range(6)])
    pts = Rot([cx.sb([128, 2, 512], BF16) for _ in range(4)])
    den = cx.sb([128, 512], F32)
    aos = Rot([cx.sb([64, 512], BF16) for _ in range(3)])
    sps = Rot([cx.ps([128, 1024]) for _ in range(2)])
    opg = [cx.ps([128, 512]) for _ in range(3)]
    cnt = 0
    for m in range(DIL_NM):
        dma(P, "sp", mk[:, m, :], dmask_d.ap()[:, m, :], [], [mk.r[m]])
    vview = vv.ap().rearrange("(kt p) n -> p kt n", p=128)
    cnt = 0
    for slot in range(2):
        if slot == 1:
            pass
        for g in range(3):
            hl = g * 2 + slot
            for q4 in range(4):
                sl = slice(q4 * 2048, (q4 + 1) * 2048)
                dma(P, "pool", Ks[g][0:64, sl], kT.ap()[hl * 64:(hl + 1) * 64, sl], [ires], [Ks[g].r[0]])
                dma(P, "pool", Vs[g][:, q4 * 16:(q4 + 1) * 16, 0:64], vview[:, q4 * 16:(q4 + 1) * 16, hl * 64:(hl + 1) * 64], [ires], [Vs[g].r[0]])
        if slot == 0:
            for b_ in Vs:
                P.op("pool", lambda e, b_=b_: e.memset(b_[:, :, 64:128], 1.0), [], [b_.r[1]])
            for b_ in Ks:
                P.op("pool", lambda e, b_=b_: e.memset(b_[64:128, :], 0.0), [], [b_.r[1]])
            for b_ in qts.tiles:
                P.op("pool", lambda e, b_=b_: e.memset(b_[64:128, :], 0.0), [], [b_.r[1]])

        def load_q(I, slot=slot):
            res = []
            for g in range(3):
                hl = g * 2 + slot
                qt = qts.next()
                dma(P, "pool", qt[0:64, :], qT.ap()[hl * 64:(hl + 1) * 64, I * 512:(I + 1) * 512], [ires], [qt.r[0]])
                res.append(qt)
            return res
        nxt_q = load_q(0)
        for I in range(NT):
            work = []
            cur_q = nxt_q
            if I + 1 < NT:
                nxt_q = load_q(I + 1)
            for g in range(3):
                hl = g * 2 + slot
                d = DIL_D[g]
                qt = cur_q[g]
                kts = [kt for kt in range(4 * I - d, 4 * I + 4) if kt >= 0]
                for c0 in range(0, len(kts), 2):
                    work.append((g, d, qt, kts, kts[c0:c0 + 2]))

            def emit_qk(w, I=I):
                g, d, qt, kts, grp = w
                n = len(grp)
                sp_ = sps.next()

                def smm(e, sp_=sp_, grp=grp, qt=qt, g=g):
                    for j, kt in enumerate(grp):
                        ins = e.matmul(sp_[:, j * 512:(j + 1) * 512], Ks[g][:, kt * 128:(kt + 1) * 128], qt[:, :], start=True, stop=True)
                    return ins
                P.op("pe", smm, [Ks[g].r[0], Ks[g].r[1], qt.r[0], qt.r[1]], [sp_.r[0]])
                pt = pts.next()
                P.op("act", lambda e, pt=pt, sp_=sp_, n=n: e.activation(out=pt[:, 0:n, :].rearrange("p a b -> p (a b)"),
                                                                        in_=sp_[:, 0:n * 512], func=AF.Exp), [], [pt.r[0], sp_.r[0]])
                m0 = DIL_MOFF[g] + (grp[0] - (4 * I - d))
                P.op("dve", lambda e, pt=pt, m0=m0, n=n: e.tensor_tensor(out=pt[:, 0:n, :], in0=pt[:, 0:n, :], in1=mk[:, m0:m0 + n, :], op=ALU.mult),
                     [mk.r[m0 + j] for j in range(n)], [pt.r[0]])
                return pt

            def emit_pv(w, pt):
                g, d, qt, kts, grp = w
                op_ = opg[g]

                def pv(e, op_=op_, pt=pt, grp=grp, kts=kts, g=g):
                    for j, kt in enumerate(grp):
                        ins = e.matmul(op_[:, :], Vs[g][:, kt, :], pt[:, j, :], start=(kt == kts[0]), stop=(kt == kts[-1]))
                    return ins
                P.op("pe", pv, [Vs[g].r[0], Vs[g].r[1], pt.r[0]], [op_.r[0]])

            LA = 2
            pend = [emit_qk(work[wi]) for wi in range(min(LA, len(work)))]
            for wi in range(len(work)):
                if wi + LA < len(work):
                    pend.append(emit_qk(work[wi + LA]))
                emit_pv(work[wi], pend.pop(0))
            P.op("act", lambda e: e.activation(out=den[64:128, :], in_=opg[0][64:128, :], func=AF.Copy), [], [den.r[0], opg[0].r[0]])
            P.op("dve", lambda e: e.tensor_tensor(out=den[64:128, :], in0=den[64:128, :], in1=opg[1][64:128, :], op=ALU.add), [], [den.r[0], opg[1].r[0]])
            P.op("dve", lambda e: e.tensor_tensor(out=den[64:128, :], in0=den[64:128, :], in1=opg[2][64:128, :], op=ALU.add), [], [den.r[0], opg[2].r[0]])
            P.op("dve", lambda e: e.reciprocal(out=den[64:128, :], in_=den[64:128, :]), [], [den.r[0]])
            for g in range(3):
                hl = g * 2 + slot
                ao = aos.next()
                P.op("dve", lambda e, ao=ao, g=g: e.tensor_tensor(out=ao[:, :], in0=opg[g][0:64, :], in1=den[64:128, :], op=ALU.mult),
                     [den.r[0]], [ao.r[0], opg[g].r[0]])
                dma(P, "sp", attn_dst(hl, I), ao[:, :], [ao.r[0]], [ores])


def dil_consts():
    k = np.arange(128)[:, None]
    q = np.arange(512)[None, :]
    tiles = []
    for g, d in enumerate(DIL_D):
        for delta in range(d + 4):
            diff = q - k + (d - delta) * 128
            tiles.append(((diff >= 0) & (diff <= 128 * d) & (diff % d == 0)).astype(np.float32))
    return np.ascontiguousarray(np.stack(tiles, axis=1))


RG = [[0, 1], [2, 3], [4, 5], [6, 7]]


def all_gather(nc, srcs, dsts):
    sems = []
    for _ in srcs:
        _UID[0] += 1
        sems.append(nc.alloc_semaphore("cc%d" % _UID[0]))
    with nc.Block() as block:
        @block.gpsimd
        def _(g):
            for src, dst, sem in zip(srcs, dsts, sems):
                g.collective_compute("AllGather", ALU.bypass, replica_groups=RG, ins=[src.ap()], outs=[dst.ap()]).then_inc(sem)
            for sem in sems:
                g.wait_ge(sem, 1)


def build_program():
    nc = bass.Bass("TRN2", target_bir_lowering=False)
    ei = lambda name, shape, dt=F32: nc.dram_tensor(name, list(shape), dt, kind="ExternalInput")
    xT = ei("xT", [D, S])
    xres = ei("xres", [D, HALF])
    wqkv = [ei("wqkv0", [D, 3 * 512]), ei("wqkv1", [D, 3 * 384])]
    wo = [ei("wo0", [1024, D]), ei("wo1", [768, D])]
    win = [ei("win0", [D, DFF]), ei("win1", [D, DFF])]
    wout = [ei("wout0", [DFF, D]), ei("wout1", [DFF, D])]
    lnp = [ei("lnp0", [128, 32]), ei("lnp1", [128, 32])]
    cos_d, sin_d, rot_d = ei("cos", [128, S]), ei("sin", [128, S]), ei("rot", [128, 128])
    onehot_d, tri_d, ident_d = ei("onehot", [32, S], BF16), ei("tri", [128, 4, 512], BF16), ei("ident", [128, 128], BF16)
    gconst_d, dmask_d = ei("gconst", [128, 3, 2048]), ei("dmask", [128, DIL_NM, 512], BF16)
    hsel_d = ei("hsel", [128, 2])
    yT = nc.dram_tensor("yT", [D, HALF], F32, kind="ExternalOutput")
    it = lambda name, shape, dt=BF16: nc.dram_tensor(name, list(shape), dt)
    nhs = (8, 6)
    qT = [it("qT%d" % l, [nhs[l] * 64, S]) for l in range(2)]
    kT = [it("kT%d" % l, [nhs[l] * 64, S]) for l in range(2)]
    vv = [it("v%d" % l, [S, nhs[l] * 64]) for l in range(2)]
    CH = 256
    a_in = [[it("ain%d_%d" % (l, j), [CH, HALF]) for j in range(2 * nhs[l] * 64 // CH)] for l in range(2)]
    a_g = [[it("ag%d_%d" % (l, j), [2 * CH, HALF]) for j in range(2 * nhs[l] * 64 // CH)] for l in range(2)]
    wp = [it("wp%d" % l, [9, 128, 8192]) for l in range(2)]
    res1 = it("res1", [D, HALF], F32)
    hg_in = [it("hgin%d" % j, [CH, HALF]) for j in range(D // CH)]
    hg_out = [it("hgout%d" % j, [2 * CH, HALF]) for j in range(D // CH)]

    xv = xT.ap().rearrange("(c p) s -> p c s", p=128)
    xrv = xres.ap().rearrange("(c p) s -> p c s", p=128)
    r1v = res1.ap().rearrange("(c p) s -> p c s", p=128)
    hgi = [h_.ap().rearrange("(c p) s -> p c s", p=128) for h_ in hg_in]
    hgo = [h_.ap().rearrange("(r c p) s -> r p c s", r=2, p=128) for h_ in hg_out]
    yv = yT.ap().rearrange("(c p) s -> p c s", p=128)
    tile_sl = lambda v: (lambda tt: v[:, :, tt * 512:(tt + 1) * 512])

    import os
    pool = SemPool(nc)
    nstage = int(os.environ.get("DBG_STAGES", "99"))
    stage = [0]

    def go():
        stage[0] += 1
        return stage[0] <= nstage
    for l in range(2):
        nh = nhs[l]
        nk = 2 * nh * 64 // 128
        if not go():
            break
        cx = Ctx(nc)
        with cx.st:
            hook = None
            if l == 0:
                hsrc, f32 = (lambda t: [(0, 8, xv[:, :, t * 512:(t + 1) * 512])]), True
            else:
                hsrc, f32 = (lambda t: [(2 * j, 2 * j + 2, hgo[j][t // 8][:, :, (t % 8) * 512:(t % 8 + 1) * 512]) for j in range(4)]), False
            phase_a(cx, hsrc, f32, wqkv[l], cos_d, sin_d, rot_d, qT[l], kT[l], vv[l], nh, Res(), Res(), hook=hook)
            cx.P.emit(pool)
        if not go():
            break
        cx = Ctx(nc)
        with cx.st:
            nhd = nh * 64
            def dst(h, I, l=l, nhd=nhd):
                row0 = (I // 8) * nhd + h * 64
                return a_in[l][row0 // CH].ap()[row0 % CH:row0 % CH + 64, (I % 8) * 512:(I % 8 + 1) * 512]
            if l == 0:
                stg = Rot([cx.sb([128, 8192], BF16, nres=4) for _ in range(2)])
                pc = [precast_weights(cx, wo[ll], win[ll], wout[ll], wp[ll], Res(), nk=2 * nhs[ll] * 64 // 128, stg=stg) for ll in range(2)]
                todo = [(ll, i) for ll in range(2) for i in range(9)]

                def mhook(h, pc=pc, todo=todo):
                    for (ll, i) in todo[h * 3:(h + 1) * 3]:
                        pc[ll](i)
                phase_b_moba(cx, qT[l], kT[l], vv[l], onehot_d, tri_d, ident_d, gconst_d, dst, nh, Res(), Res(), hook=mhook)
            else:
                phase_b_dil(cx, qT[l], kT[l], vv[l], dmask_d, dst, Res(), Res())
            cx.P.emit(pool)
        all_gather(nc, a_in[l], a_g[l])
        if not go():
            break
        cx = Ctx(nc)
        with cx.st:
            nhd = nh * 64
            kh = nk // 2

            def attn_src(tt, l=l, nhd=nhd, kh=kh, nk=nk):
                res = []
                for hh in range(2):
                    lst = []
                    for k in range(nk):
                        row0 = hh * nhd + (k % kh) * 128
                        rr = (k // kh) * CH + row0 % CH
                        lst.append(a_g[l][row0 // CH].ap()[rr:rr + 128, tt * 512:(tt + 1) * 512])
                    res.append(lst)
                return res
            if l == 0:
                phase_c(cx, attn_src, tile_sl(xrv), wp[l], lnp[l], tile_sl(r1v), (lambda tt: [(2 * j, 2 * j + 2, hgi[j][:, :, tt * 512:(tt + 1) * 512]) for j in range(4)]), Res(), Res(), Res(), Res(), nk=nk, hsel_d=hsel_d)
            else:
                phase_c(cx, attn_src, tile_sl(r1v), wp[l], lnp[l], tile_sl(yv), None, Res(), Res(), Res(), Res(), nk=nk, hsel_d=hsel_d)
            cx.P.emit(pool)
        if l == 0:
            all_gather(nc, hg_in, hg_out)
    return nc


def _lnp(g1, b1, g2, b2):
    return np.ascontiguousarray(np.concatenate([np.asarray(v, np.float32).reshape(8, 128).T for v in (g1, b1, g2, b2)], axis=1))


def make_in_maps(x, moba_w_qkv, moba_w_o, dil_w_qkv, dil_w_o, mlp_w_in, mlp_w_out, ln_mix_g, ln_mix_b, ln_mlp_g, ln_mlp_b):
    f = lambda a: np.ascontiguousarray(np.asarray(a, dtype=np.float32))
    cos128, sin128, R = rope_tables()
    onehot, tri, ident, gconst = moba_consts()
    dmask = dil_consts()
    wq0 = f(moba_w_qkv)[0]
    wq1 = f(dil_w_qkv)[0]
    shared = {"win0": f(mlp_w_in)[0], "win1": f(mlp_w_in)[1], "wout0": f(mlp_w_out)[0], "wout1": f(mlp_w_out)[1],
              "lnp0": _lnp(ln_mix_g[0], ln_mix_b[0], ln_mlp_g[0], ln_mlp_b[0]),
              "lnp1": _lnp(ln_mix_g[1], ln_mix_b[1], ln_mlp_g[1], ln_mlp_b[1]),
              "cos": cos128, "sin": sin128, "rot": R, "onehot": onehot.astype(ml_dtypes.bfloat16), "tri": tri.astype(ml_dtypes.bfloat16),
              "ident": ident.astype(ml_dtypes.bfloat16), "gconst": gconst, "dmask": dmask.astype(ml_dtypes.bfloat16),
              "wo0": f(moba_w_o)[0]}
    heads1 = [[g * 4 + 2 * r + sl for g in range(3) for sl in range(2)] for r in range(2)]
    rows1 = np.concatenate([np.arange(h * 64, (h + 1) * 64) for r in range(2) for h in heads1[r]])
    shared["wo1"] = np.ascontiguousarray(f(dil_w_o)[0][rows1])
    maps = []
    xf = f(x)
    for c in range(8):
        b, r = c // 2, c % 2
        xT = np.ascontiguousarray(xf[b].T)
        m = dict(shared)
        m["xT"] = xT
        hs = np.zeros((128, 2), np.float32)
        hs[:, r] = 1.0
        m["hsel"] = hs
        m["xres"] = np.ascontiguousarray(xT[:, r * HALF:(r + 1) * HALF])
        cols0 = np.concatenate([np.arange(part * 1024 + (8 * r) * 64, part * 1024 + (8 * r + 8) * 64) for part in range(3)])
        m["wqkv0"] = np.ascontiguousarray(wq0[:, cols0])
        cols1 = np.concatenate([np.arange(part * 768 + h * 64, part * 768 + (h + 1) * 64) for part in range(3) for h in heads1[r]])
        m["wqkv1"] = np.ascontiguousarray(wq1[:, cols1])
        maps.append(m)
    return maps


def kernel(**inputs):
    nc = build_program()
    maps = make_in_maps(**inputs)
    res = run_bass_kernel_spmd(nc, maps, core_ids=list(range(8)))
    out = np.empty((B, S, D), np.float32)
    for c in range(8):
        b, r = c // 2, c % 2
        out[b, r * HALF:(r + 1) * HALF, :] = np.asarray(res.results[c]["yT"]).T
    return out
```

```python
import numpy as np
import ml_dtypes
import concourse.bass as bass
import concourse.mybir as mybir
from concourse.bass_utils import run_bass_kernel_spmd

F32 = mybir.dt.float32
BF16 = mybir.dt.bfloat16
AF = mybir.ActivationFunctionType
ALU = mybir.AluOpType
AX = mybir.AxisListType

D = 1024
S = 8192
B = 4
DFF = 4096
DH = 64
NT = S // 512
HALF = S // 2
LN_EPS = 1e-5
ALPHA = (2.0 * 2) ** 0.25
NEG = -30000.0
DIL = ((128, 1), (512, 4), (2048, 16))


class Res:
    __slots__ = ("w", "rc", "rd")

    def __init__(self):
        self.w = None
        self.rc = {}
        self.rd = []


class Op:
    __slots__ = ("eng", "fn", "deps", "sig", "val", "dma", "sem")


class Prog:
    ENGS = ("pe", "act", "dve", "pool", "sp")
    NDSEM = 12

    def __init__(self, nc):
        self.nc = nc
        self.ops = {e: [] for e in self.ENGS}
        self.dma_hist = {e: [None] * self.NDSEM for e in ("sp", "pool", "act")}
        self.dma_cnt = {e: 0 for e in ("sp", "pool", "act")}
        self.dma_val = {e: [0] * self.NDSEM for e in ("sp", "pool", "act")}

    def op(self, eng, fn, reads=(), writes=(), dma=False):
        o = Op()
        o.eng, o.fn, o.dma, o.sig, o.val, o.sem = eng, fn, dma, False, 0, None
        deps = {}
        for r in reads:
            if r.w is not None:
                deps[id(r.w)] = r.w
        for w in writes:
            if w.w is not None:
                deps[id(w.w)] = w.w
            for x in w.rc.values():
                deps[id(x)] = x
            for x in w.rd:
                deps[id(x)] = x
        if dma:
            k = self.dma_cnt[eng] % self.NDSEM
            self.dma_cnt[eng] += 1
            prev = self.dma_hist[eng][k]
            if prev is not None:
                deps[id(prev)] = prev
            self.dma_hist[eng][k] = o
            self.dma_val[eng][k] += 16
            o.sem = (eng, k)
            o.val = self.dma_val[eng][k]
        o.deps = []
        for d in deps.values():
            if d is o:
                continue
            if (not d.dma) and (not dma) and d.eng == "pe" and eng == "pe":
                continue
            if not d.dma:
                d.sig = True
            o.deps.append(d)
        for r in reads:
            if dma:
                r.rd.append(o)
            else:
                r.rc[eng] = o
        for w in writes:
            w.w = o
            w.rc = {}
            w.rd = []
        self.ops[eng].append(o)
        return o

    def emit(self, pool=None):
        nc = self.nc
        if pool is None:
            pool = SemPool(nc)
        for e in self.ENGS:
            c = pool.ebase[e]
            for o in self.ops[e]:
                if o.sig and not o.dma:
                    c += 1
                    o.val = c
            pool.ebase[e] = c
        for e in self.ENGS:
            for o in self.ops[e]:
                if o.dma:
                    o.val += pool.dbase[o.sem]
        esem, dsem = pool.esem, pool.dsem
        with nc.Block() as block:
            def run(e, eng):
                waited = {}
                for o in self.ops[e]:
                    for d in o.deps:
                        s = dsem[d.sem] if d.dma else esem[d.eng]
                        key = d.sem if d.dma else d.eng
                        if waited.get(key, 0) < d.val:
                            eng.wait_ge(s, d.val)
                            waited[key] = d.val
                    ins = o.fn(eng)
                    if o.dma:
                        ins.then_inc(dsem[o.sem], 16)
                    elif o.sig:
                        ins.then_inc(esem[e], 1)
                if e in ("sp", "pool") and self.dma_cnt[e]:
                    for k in range(min(self.NDSEM, self.dma_cnt[e])):
                        eng.wait_ge(dsem[(e, k)], pool.dbase[(e, k)] + self.dma_val[e][k])

            @block.tensor
            def _(t):
                run("pe", t)

            @block.scalar
            def _(s):
                run("act", s)

            @block.vector
            def _(v):
                run("dve", v)

            @block.gpsimd
            def _(g):
                run("pool", g)

            @block.sync
            def _(sy):
                run("sp", sy)
        for e in ("sp", "pool"):
            for k in range(self.NDSEM):
                pool.dbase[(e, k)] += self.dma_val[e][k]


class SemPool:
    def __init__(self, nc):
        _UID[0] += 1
        u = "_%d" % _UID[0]
        self.nc = nc
        self.esem = {e: nc.alloc_semaphore("es_" + e + u) for e in Prog.ENGS}
        self.dsem = {(q, k): nc.alloc_semaphore("ds_%s%d%s" % (q, k, u)) for q in ("sp", "pool") for k in range(Prog.NDSEM)}
        self.ebase = {e: 0 for e in Prog.ENGS}
        self.dbase = {key: 0 for key in self.dsem}


_UID = [0]


class Tile:
    def __init__(self, t, nres=1):
        self.t = t
        self.r = [Res() for _ in range(nres)]

    def __getitem__(self, k):
        return self.t[k]


class Ctx:
    def __init__(self, nc):
        import contextlib
        self.nc = nc
        self.P = Prog(nc)
        self.st = contextlib.ExitStack()
        self.n = 0

    def sb(self, shape, dt, nres=1):
        _UID[0] += 1
        return Tile(self.st.enter_context(self.nc.sbuf_tensor("sb%d" % _UID[0], list(shape), dt)), nres)

    def ps(self, shape, dt=F32, nres=1):
        _UID[0] += 1
        return Tile(self.st.enter_context(self.nc.psum_tensor("ps%d" % _UID[0], list(shape), dt)), nres)

    def dram(self, name, shape, dt, kind="Internal"):
        return self.nc.dram_tensor(name, list(shape), dt, kind=kind)


class Rot:
    def __init__(self, tiles):
        self.tiles = tiles
        self.i = 0

    def next(self):
        t = self.tiles[self.i % len(self.tiles)]
        self.i += 1
        return t


def dma(P, q, out, in_, reads, writes):
    def f(e):
        return e.dma_start(out=out(e) if callable(out) else out, in_=in_(e) if callable(in_) else in_)
    return P.op(q, f, reads=reads, writes=writes, dma=True)


def phase_a(cx, hsrc, h_is_f32, wqkv, cos_d, sin_d, rotm_d, qT, kT, vv, nh, hres, ores, hook=None):
    nc, P = cx.nc, cx.P
    npair = nh // 2
    ncol = nh * DH
    w_sb = cx.sb([128, 8, 3 * ncol], BF16, nres=8)
    rot_sb = cx.sb([128, 128], BF16)
    wv = wqkv.ap().rearrange("(c p) n -> p c n", p=128)
    for c in range(8):
        dma(P, "pool", w_sb[:, c, :], wv[:, c, :], [], [w_sb.r[c]])
    dma(P, "pool", rot_sb[:, :], rotm_d.ap(), [], [rot_sb.r[0]])
    hts = Rot([cx.sb([128, 8, 512], BF16) for _ in range(2)])
    cs = Rot([cx.sb([128, 2, 512], F32, nres=2) for _ in range(2)])
    tbs = Rot([cx.sb([128, 512], BF16) for _ in range(3)])
    tmp1 = Rot([cx.sb([128, 512], F32) for _ in range(2)])
    tmp2 = Rot([cx.sb([128, 512], F32) for _ in range(2)])
    outs = Rot([cx.sb([128, 512], BF16) for _ in range(3)])
    vouts = Rot([cx.sb([128, ncol], BF16) for _ in range(2)])
    psA = Rot([cx.ps([128, 512]) for _ in range(3)])
    psR = Rot([cx.ps([128, 512]) for _ in range(2)])
    psV = Rot([cx.ps([128, ncol]) for _ in range(2)])
    cosv = cos_d.ap()
    sinv = sin_d.ap()
    def load_h(t):
        ht = hts.next()
        for (c0, c1, hap) in hsrc(t):
            dma(P, "pool", ht[:, c0:c1, :], hap, [hres], [ht.r[0]])
        ct = cs.next()
        dma(P, "pool", ct[:, 0, :], cosv[:, t * 512:(t + 1) * 512], [], [ct.r[0]])
        dma(P, "pool", ct[:, 1, :], sinv[:, t * 512:(t + 1) * 512], [], [ct.r[1]])
        return ht, ct
    nxt_h = load_h(0)
    for t in range(NT):
        ht, ct = nxt_h
        if t + 1 < NT:
            nxt_h = load_h(t + 1)
        if hook is not None:
            hook(t)
        pend = None
        jobs = [(which, p) for which in range(2) for p in range(npair)]
        for (which, p) in jobs + [(None, None)]:
            cur = None
            if (which, p) is not None and p is not None:
                pass
            if which is not None:
                dst = qT if which == 0 else kT
                col0 = which * ncol + p * 128
                pa = psA.next()

                def mm(e, pa=pa, ht=ht, col0=col0):
                    for c in range(8):
                        ins = e.matmul(pa[:, :], w_sb[:, c, col0:col0 + 128], ht[:, c, :], start=(c == 0), stop=(c == 7))
                    return ins
                P.op("pe", mm, list(w_sb.r) + [ht.r[0]], [pa.r[0]])
                tb = tbs.next()
                P.op("act", lambda e, tb=tb, pa=pa: e.activation(out=tb[:, :], in_=pa[:, :], func=AF.Copy),
                     [], [tb.r[0], pa.r[0]])
                cur = (which, p, dst, pa, tb)
            if pend is not None:
                pw, pp, pdst, ppa, ptb = pend
                pr = psR.next()
                P.op("pe", lambda e, pr=pr, tb=ptb: e.matmul(pr[:, :], rot_sb[:, :], tb[:, :], start=True, stop=True),
                     [rot_sb.r[0], ptb.r[0]], [pr.r[0]])
                t1 = tmp1.next()
                t2 = tmp2.next()
                sc = 0.125 if pw == 0 else 1.0
                P.op("dve", lambda e, t1=t1, pa=ppa, ct=ct, sc=sc: e.scalar_tensor_tensor(
                    out=t1[:, :], in0=pa[:, :], scalar=sc, in1=ct[:, 0, :], op0=ALU.mult, op1=ALU.mult),
                    [ct.r[0]], [t1.r[0], ppa.r[0]])
                P.op("dve", lambda e, t2=t2, pr=pr, ct=ct, sc=sc: e.scalar_tensor_tensor(
                    out=t2[:, :], in0=pr[:, :], scalar=sc, in1=ct[:, 1, :], op0=ALU.mult, op1=ALU.mult),
                    [ct.r[1]], [t2.r[0], pr.r[0]])
                ob = outs.next()
                P.op("pool", lambda e, ob=ob, t1=t1, t2=t2: e.tensor_tensor(out=ob[:, :], in0=t1[:, :], in1=t2[:, :], op=ALU.add),
                     [t1.r[0], t2.r[0]], [ob.r[0]])
                dma(P, "sp", pdst.ap()[pp * 128:(pp + 1) * 128, t * 512:(t + 1) * 512], ob[:, :], [ob.r[0]], [ores])
            pend = cur
        for s4 in range(4):
            pv = psV.next()

            def mmv(e, pv=pv, ht=ht, s4=s4):
                for c in range(8):
                    ins = e.matmul(pv[:, :], ht[:, c, s4 * 128:(s4 + 1) * 128], w_sb[:, c, 2 * ncol:3 * ncol],
                                   start=(c == 0), stop=(c == 7))
                return ins
            P.op("pe", mmv, list(w_sb.r) + [ht.r[0]], [pv.r[0]])
            vo = vouts.next()
            P.op("act", lambda e, vo=vo, pv=pv: e.activation(out=vo[:, :], in_=pv[:, :], func=AF.Copy),
                 [], [vo.r[0], pv.r[0]])
            r0 = t * 512 + s4 * 128
            dma(P, "sp", vv.ap()[r0:r0 + 128, :], vo[:, :], [vo.r[0]], [ores])


def rope_tables():
    inv = (1.0 / (10000.0 ** (np.arange(0, DH, 2, dtype=np.float32) / DH))).astype(np.float32)
    ang = (np.arange(S, dtype=np.float32)[:, None] * inv[None, :]).astype(np.float32)
    cos = np.cos(ang).astype(np.float32).T
    sin = np.sin(ang).astype(np.float32).T
    cos128 = np.ascontiguousarray(np.tile(cos, (4, 1)))
    sin128 = np.ascontiguousarray(np.tile(sin, (4, 1)))
    R = np.zeros((128, 128), np.float32)
    for hh in range(2):
        for j in range(32):
            R[hh * 64 + 32 + j, hh * 64 + j] = -1.0
            R[hh * 64 + j, hh * 64 + 32 + j] = 1.0
    return cos128, sin128, R


def precast_weights(cx, wo_d, win_d, wout_d, wp, wres, nk=8, stg=None):
    P = cx.P
    if stg is None:
        stg = Rot([cx.sb([128, 8192], BF16, nres=4) for _ in range(2)])
    wov = wo_d.ap().rearrange("(k p) d -> p k d", p=128)
    winv = win_d.ap().rearrange("(k p) f -> p k f", p=128)
    woutv = wout_d.ap().rearrange("(f p) d -> p f d", p=128)

    def piece(i):
        st = stg.next()
        if i == 0:
            src, view = wov, st.t[:, 0:nk * 1024].rearrange("p (k d) -> p k d", k=nk)
        elif i <= 4:
            j = i - 1
            src, view = winv[:, :, j * 1024:(j + 1) * 1024], st.t[:, :].rearrange("p (k f) -> p k f", k=8)
        else:
            j = i - 5
            src, view = woutv[:, :, j * 256:(j + 1) * 256], st.t[:, :].rearrange("p (f d) -> p f d", f=32)
        nsp = 2
        n0 = view.shape[1] // nsp
        for q in range(nsp):
            dma(P, "pool", view[:, q * n0:(q + 1) * n0, :], src[:, q * n0:(q + 1) * n0, :], [], [st.r[q]])
        dma(P, "sp", wp.ap()[i], st[:, :], list(st.r), [wres])
    return piece


def phase_c(cx, attn_src, res_src, wp, lnp_d, out_f32_dst, out_bf16_dst, ares, rres, wres, ores, nk=8, hsel_d=None):
    P = cx.P
    ones = cx.sb([128, 128], BF16)
    P.op("pool", lambda e: e.memset(ones[:, :], 1.0), [], [ones.r[0]])
    lnp = cx.sb([128, 32], F32)
    dma(P, "sp", lnp[:, :], lnp_d.ap(), [], [lnp.r[0]])
    hsel = cx.sb([128, 2], F32)
    if hsel_d is not None:
        dma(P, "sp", hsel[:, :], hsel_d.ap(), [], [hsel.r[0]])
    wsl = Rot([cx.sb([128, 8192], BF16) for _ in range(4)])
    ats = Rot([cx.sb([128, 8, 512], BF16, nres=8) for _ in range(1)])
    ob16 = cx.sb([128, 8, 512], BF16, nres=8)
    Rs = Rot([cx.sb([128, 8, 512], F32, nres=8) for _ in range(2)])
    hb = cx.sb([128, 8, 512], BF16, nres=8)
    u = cx.sb([128, 32, 512], BF16, nres=32)
    zb = cx.sb([128, 8, 512], BF16, nres=8)
    zq = cx.sb([128, 8, 512], BF16, nres=8)
    mt = cx.sb([128, 512], F32)
    msq = cx.sb([128, 512], F32)
    rstd = cx.sb([128, 512], F32)
    rl = Rot([cx.sb([128, 512], F32) for _ in range(2)])
    acc = Rot([cx.ps([128, 512]) for _ in range(4)])
    st1 = cx.ps([128, 512])
    st2 = cx.ps([128, 512])
    wpv = wp.ap()

    def load_piece(i):
        w = wsl.next()
        for q in range(2):
            dma(P, "pool", w[:, q * 4096:(q + 1) * 4096], wpv[i][:, q * 4096:(q + 1) * 4096], [wres], [w.r[0]])
        return w

    def ln_pre(R):
        allr = list(R.r)
        P.op("act", lambda e: e.activation(out=zb[:, :, :], in_=R[:, :, :], func=AF.Copy), allr, list(zb.r))
        P.op("act", lambda e: e.activation(out=zq[:, :, :], in_=R[:, :, :], func=AF.Square), allr, list(zq.r))

    def ln_stats():
        def s1(e):
            for c in range(8):
                ins = e.matmul(st1[:, :], ones[:, :], zb[:, c, :], start=(c == 0), stop=(c == 7))
            return ins

        def s2(e):
            for c in range(8):
                ins = e.matmul(st2[:, :], ones[:, :], zq[:, c, :], start=(c == 0), stop=(c == 7))
            return ins
        P.op("pe", s1, [ones.r[0]] + list(zb.r), [st1.r[0]])
        P.op("pe", s2, [ones.r[0]] + list(zq.r), [st2.r[0]])

    def ln_post(R, gi, bi, bft, f32_dst, bf_dst):
        allr = list(R.r)
        P.op("act", lambda e: e.activation(out=mt[:, :], in_=st1[:, :], func=AF.Copy, scale=1.0 / D), [], [mt.r[0], st1.r[0]])
        P.op("dve", lambda e: e.tensor_tensor(out=msq[:, :], in0=mt[:, :], in1=mt[:, :], op=ALU.mult), [mt.r[0]], [msq.r[0]])
        P.op("dve", lambda e: e.scalar_tensor_tensor(out=rstd[:, :], in0=st2[:, :], scalar=1.0 / D, in1=msq[:, :],
                                                      op0=ALU.mult, op1=ALU.subtract), [msq.r[0]], [rstd.r[0], st2.r[0]])
        P.op("dve", lambda e: e.tensor_scalar_add(out=rstd[:, :], in0=rstd[:, :], scalar1=LN_EPS), [], [rstd.r[0]])
        P.op("act", lambda e: e.activation(out=rstd[:, :], in_=rstd[:, :], func=AF.Sqrt), [], [rstd.r[0]])
        P.op("dve", lambda e: e.reciprocal(out=rstd[:, :], in_=rstd[:, :]), [], [rstd.r[0]])
        for c in range(8):
            P.op("dve", lambda e, c=c: e.tensor_tensor(out=R[:, c, :], in0=R[:, c, :], in1=mt[:, :], op=ALU.subtract),
                 [mt.r[0]], [R.r[c]])
            P.op("dve", lambda e, c=c: e.tensor_tensor(out=R[:, c, :], in0=R[:, c, :], in1=rstd[:, :], op=ALU.mult),
                 [rstd.r[0]], [R.r[c]])
            if bft is not None:
                P.op("act", lambda e, c=c: e.activation(out=bft[:, c, :], in_=R[:, c, :], func=AF.Identity,
                                                         scale=lnp[:, gi * 8 + c:gi * 8 + c + 1], bias=lnp[:, bi * 8 + c:bi * 8 + c + 1]),
                     [lnp.r[0], R.r[c]], [bft.r[c]])
            P.op("act", lambda e, c=c: e.activation(out=R[:, c, :], in_=R[:, c, :], func=AF.Identity,
                                                     scale=lnp[:, gi * 8 + c:gi * 8 + c + 1], bias=lnp[:, bi * 8 + c:bi * 8 + c + 1]),
                 [lnp.r[0]], [R.r[c]])
        if f32_dst is not None:
            dma(P, "sp", f32_dst, R[:, :, :], allr, [ores])
        if bf_dst is not None:
            for (c0, c1, dap) in bf_dst:
                dma(P, "sp", dap, bft[:, c0:c1, :], list(bft.r[c0:c1]), [ores])

    ntile = HALF // 512
    Rt = {}

    def emit_wo(tt):
        at = ats.next()
        R = Rs.next()
        Rt[tt] = R
        if hsel_d is None:
            dma(P, "sp", at[:, 0:nk, :], attn_src(tt), [ares], list(at.r))
        else:
            s0, s1 = attn_src(tt)
            for k in range(nk):
                dma(P, "sp", zb[:, k, :], s0[k], [ares], [zb.r[k]])
                dma(P, "sp", zq[:, k, :], s1[k], [ares], [zq.r[k]])
            P.op("act", lambda e: e.activation(out=zq[:, 0:nk, :], in_=zq[:, 0:nk, :], func=AF.Identity, scale=hsel[:, 1:2]), [hsel.r[0]], list(zq.r))
            P.op("dve", lambda e, at=at: e.scalar_tensor_tensor(out=at[:, 0:nk, :], in0=zb[:, 0:nk, :], scalar=hsel[:, 0:1], in1=zq[:, 0:nk, :],
                                                             op0=ALU.mult, op1=ALU.add), [hsel.r[0]] + list(zb.r) + list(zq.r), list(at.r))
        dma(P, "sp", R[:, :, :], res_src(tt), [rres], list(R.r))
        w = load_piece(0)
        wv = w.t[:, 0:nk * 1024].rearrange("p (k d) -> p k d", k=nk)
        for c in range(8):
            pa = acc.next()

            def mmo(e, pa=pa, c=c, wv=wv, at=at):
                for k in range(nk):
                    ins = e.matmul(pa[:, :], wv[:, k, c * 128:(c + 1) * 128], at[:, k, :], start=(k == 0), stop=(k == nk - 1))
                return ins
            P.op("pe", mmo, [w.r[0]] + list(at.r), [pa.r[0]])
            P.op("dve", lambda e, pa=pa, c=c, R=R: e.scalar_tensor_tensor(out=R[:, c, :], in0=R[:, c, :], scalar=ALPHA, in1=pa[:, :],
                                                                      op0=ALU.mult, op1=ALU.add), [], [R.r[c], pa.r[0]])
        ln_pre(R)

    def emit_up(tt, j):
        w = load_piece(1 + j)
        wv = w.t[:, :].rearrange("p (k f) -> p k f", k=8)
        for f in range(8):
            pa = acc.next()

            def mmu(e, pa=pa, f=f, wv=wv):
                for k in range(8):
                    ins = e.matmul(pa[:, :], wv[:, k, f * 128:(f + 1) * 128], hb[:, k, :], start=(k == 0), stop=(k == 7))
                return ins
            P.op("pe", mmu, [w.r[0]] + list(hb.r), [pa.r[0]])
            r_ = rl.next()
            P.op("act", lambda e, pa=pa, r_=r_: e.activation(out=r_[:, :], in_=pa[:, :], func=AF.Relu), [], [r_.r[0], pa.r[0]])
            fi = j * 8 + f
            P.op("dve", lambda e, pa=pa, r_=r_, fi=fi: e.tensor_tensor(out=u[:, fi, :], in0=r_[:, :], in1=pa[:, :], op=ALU.mult),
                 [r_.r[0]], [u.r[fi], pa.r[0]])

    def emit_down(tt, j):
        R = Rt[tt]
        w = load_piece(5 + j)
        wv = w.t[:, :].rearrange("p (f d) -> p f d", f=32)
        for cc in range(2):
            c = 2 * j + cc
            pa = acc.next()

            def mmd(e, pa=pa, cc=cc, wv=wv):
                for f in range(32):
                    ins = e.matmul(pa[:, :], wv[:, f, cc * 128:(cc + 1) * 128], u[:, f, :], start=(f == 0), stop=(f == 31))
                return ins
            P.op("pe", mmd, [w.r[0]] + list(u.r), [pa.r[0]])
            P.op("dve", lambda e, pa=pa, c=c, R=R: e.scalar_tensor_tensor(out=R[:, c, :], in0=R[:, c, :], scalar=ALPHA, in1=pa[:, :],
                                                                      op0=ALU.mult, op1=ALU.add), [], [R.r[c], pa.r[0]])

    def ln1_finish(tt):
        ln_stats()
        ln_post(Rt[tt], 0, 1, hb, None, None)

    def ln2_stats_post(tt):
        ln_stats()
        bf = out_bf16_dst(tt) if out_bf16_dst is not None else None
        ln_post(Rt[tt], 2, 3, ob16 if bf is not None else None, out_f32_dst(tt), bf)

    emit_wo(0)
    ln1_finish(0)
    for j in range(4):
        emit_up(0, j)
    for tt in range(ntile):
        nxt = tt + 1 < ntile
        if nxt:
            emit_wo(tt + 1)
        emit_down(tt, 0)
        emit_down(tt, 1)
        if nxt:
            ln1_finish(tt + 1)
        emit_down(tt, 2)
        emit_down(tt, 3)
        ln_pre(Rt[tt])
        if nxt:
            emit_up(tt + 1, 0)
        ln2_stats_post(tt)
        if nxt:
            for j in range(1, 4):
                emit_up(tt + 1, j)


def phase_b_moba(cx, qT, kT, vv, onehot_d, tri_d, ident_d, gconst_d, attn_dst, nh, ires, ores, hook=None):
    P = cx.P
    NQT = S // 128
    Qa = Rot([cx.sb([96, S], BF16, nres=2) for _ in range(2)])
    Ka = Rot([cx.sb([96, S], BF16, nres=2) for _ in range(2)])
    Vh = Rot([cx.sb([128, 64, 128], BF16, nres=2) for _ in range(2)])
    tri = cx.sb([128, 4, 512], BF16)
    ident = cx.sb([128, 128], BF16)
    gcon = cx.sb([128, 3, 2048], F32)
    kmf = cx.sb([64, 32], F32)
    kmb = cx.sb([96, 32], BF16, nres=2)
    gs = cx.sb([128, 512], F32)
    m8 = cx.sb([128, 16, 8], F32)
    sel = cx.sb([128, 512], F32)
    bpad = cx.sb([128, 16, 96], BF16)
    pts = Rot([cx.sb([128, 2, 512], BF16) for _ in range(4)])
    rcp = Rot([cx.sb([128, 512], F32) for _ in range(2)])
    aos = Rot([cx.sb([64, 512], BF16) for _ in range(2)])
    sps = Rot([cx.ps([128, 1024]) for _ in range(3)])
    ops_ = Rot([cx.ps([128, 512]) for _ in range(2)])
    dma(P, "sp", tri[:, :, :], tri_d.ap(), [], [tri.r[0]])
    dma(P, "sp", ident[:, :], ident_d.ap(), [], [ident.r[0]])
    dma(P, "sp", gcon[:, :, :], gconst_d.ap(), [], [gcon.r[0]])
    for b_ in Ka.tiles:
        dma(P, "sp", b_[64:96, :], onehot_d.ap(), [], [b_.r[1]])
    vview = vv.ap().rearrange("(kt p) n -> p kt n", p=128)
    def load_head(h):
        qa, ka, vh = Qa.next(), Ka.next(), Vh.next()
        for q4 in range(4):
            sl = slice(q4 * 2048, (q4 + 1) * 2048)
            dma(P, "pool", qa[0:64, sl], qT.ap()[h * 64:(h + 1) * 64, sl], [ires], [qa.r[0]])
            dma(P, "pool", ka[0:64, sl], kT.ap()[h * 64:(h + 1) * 64, sl], [ires], [ka.r[0]])
        for q4 in range(4):
            dma(P, "pool", vh[:, q4 * 16:(q4 + 1) * 16, 0:64], vview[:, q4 * 16:(q4 + 1) * 16, h * 64:(h + 1) * 64], [ires], [vh.r[0]])
        return qa, ka, vh
    nxt_head = load_head(0)
    for b_ in Vh.tiles:
        P.op("pool", lambda e, b_=b_: e.memset(b_[:, :, 64:128], 1.0), [], [b_.r[1]])
    P.op("pool", lambda e: e.memset(bpad[:, :, :], 0.0), [], [bpad.r[0]])
    P.op("pool", lambda e: e.memset(kmb[64:96, :], 0.0), [], [kmb.r[1]])
    for b_ in Qa.tiles:
        P.op("pool", lambda e, b_=b_: e.memset(b_[64:96, :], 0.0), [], [b_.r[1]])
    for h in range(nh):
        qa, ka, vh = nxt_head
        if h + 1 < nh:
            nxt_head = load_head(h + 1)
        if hook is not None:
            hook(h)
        def make_prep(qa, ka):
            pieces = []

            def p_km():
                P.op("dve", lambda e, ka=ka: e.tensor_reduce(out=kmf[:, :], in_=ka[0:64, :].rearrange("p (n k) -> p n k", k=256),
                                                              axis=AX.X, op=ALU.add), [ka.r[0]], [kmf.r[0]])
                P.op("dve", lambda e: e.tensor_scalar_mul(out=kmb[0:64, :], in0=kmf[:, :], scalar1=1.0 / 256), [kmf.r[0]], [kmb.r[0]])
            pieces.append(p_km)

            def p_gate(g):
                gp = sps.next()

                def gmm(e, gp=gp, g=g, qa=qa):
                    for j in range(16):
                        qt = g * 16 + j
                        ins = e.matmul(gp[:, j * 32:(j + 1) * 32], qa[0:96, qt * 128:(qt + 1) * 128], kmb[:, :], start=True, stop=True)
                    return ins
                P.op("pe", gmm, [qa.r[0], qa.r[1], kmb.r[0], kmb.r[1]], [gp.r[0]])
                cs_ = slice(g * 512, (g + 1) * 512)
                P.op("dve", lambda e, gp=gp, cs_=cs_: e.tensor_tensor(out=gs[:, :], in0=gp[:, 0:512], in1=gcon[:, 0, cs_], op=ALU.add),
                     [gcon.r[0]], [gs.r[0], gp.r[0]])

                def mx(e):
                    for j in range(16):
                        ins = e.max(out=m8[:, j, :], in_=gs[:, j * 32:(j + 1) * 32])
                    return ins
                P.op("dve", mx, [gs.r[0]], [m8.r[0]])

                def ge(e):
                    for j in range(16):
                        ins = e.tensor_scalar(out=sel[:, j * 32:(j + 1) * 32], in0=gs[:, j * 32:(j + 1) * 32],
                                              scalar1=m8[:, j, 2:3], scalar2=None, op0=ALU.is_ge)
                    return ins
                P.op("dve", ge, [gs.r[0], m8.r[0]], [sel.r[0]])
                P.op("dve", lambda e, cs_=cs_: e.tensor_tensor(out=sel[:, :], in0=sel[:, :], in1=gcon[:, 1, cs_], op=ALU.mult),
                     [gcon.r[0]], [sel.r[0]])
                P.op("dve", lambda e, cs_=cs_: e.tensor_tensor(out=sel[:, :], in0=sel[:, :], in1=gcon[:, 2, cs_], op=ALU.add),
                     [gcon.r[0]], [sel.r[0]])
                P.op("dve", lambda e: e.tensor_scalar(out=bpad[:, :, 64:96], in0=sel[:, :].rearrange("p (j n) -> p j n", n=32),
                                                       scalar1=1.0, scalar2=-NEG, op0=ALU.subtract, op1=ALU.mult),
                     [sel.r[0]], [bpad.r[0]])
                for j4 in range(4):
                    tp = sps.next()

                    def tmm(e, tp=tp, j4=j4):
                        for jj in range(4):
                            ins = e.matmul(tp[0:96, jj * 128:(jj + 1) * 128], bpad[:, j4 * 4 + jj, :], ident[:, :], start=True, stop=True)
                        return ins
                    P.op("pe", tmm, [bpad.r[0], ident.r[0]], [tp.r[0]])
                    q0 = (g * 16 + j4 * 4) * 128
                    P.op("act", lambda e, tp=tp, q0=q0, qa=qa: e.activation(out=qa[64:96, q0:q0 + 512], in_=tp[64:96, 0:512], func=AF.Copy),
                         [], [qa.r[1], tp.r[0]])
            for g in range(4):
                pieces.append(lambda g=g: p_gate(g))
            return pieces

        for pc_ in make_prep(qa, ka):
            pc_()
        nxt_prep = []
        ins_at = {3: 0, 6: 1, 9: 2, 12: 3, 14: 4}
        groups = [(I, kt) for I in range(NT) for kt in range(0, 4 * (I + 1), 2)]
        opt = {}

        def emit_qk(gi, qa=qa, ka=ka):
            I, kt = groups[gi]
            qsl = slice(I * 512, (I + 1) * 512)
            sp_ = sps.next()

            def smm(e, sp_=sp_, kt=kt, qsl=qsl):
                for j in range(2):
                    ins = e.matmul(sp_[:, j * 512:(j + 1) * 512], ka[0:96, (kt + j) * 128:(kt + j + 1) * 128], qa[0:96, qsl],
                                   start=True, stop=True)
                return ins
            P.op("pe", smm, [qa.r[0], qa.r[1], ka.r[0], ka.r[1]], [sp_.r[0]])
            pt = pts.next()
            P.op("act", lambda e, pt=pt, sp_=sp_: e.activation(out=pt[:, :, :].rearrange("p a b -> p (a b)"), in_=sp_[:, :], func=AF.Exp),
                 [], [pt.r[0], sp_.r[0]])
            if kt >= 4 * I:
                d0 = kt - 4 * I
                P.op("dve", lambda e, pt=pt, d0=d0: e.tensor_tensor(out=pt[:, :, :], in0=pt[:, :, :], in1=tri[:, d0:d0 + 2, :], op=ALU.mult),
                     [tri.r[0]], [pt.r[0]])
            return pt

        def emit_pv(gi, pt, vh=vh, h=h):
            I, kt = groups[gi]
            nkt = 4 * (I + 1)
            if kt == 0:
                opt[I] = ops_.next()
            op_ = opt[I]

            def pv(e, op_=op_, pt=pt, kt=kt, nkt=nkt):
                for j in range(2):
                    ins = e.matmul(op_[:, :], vh[:, kt + j, :], pt[:, j, :], start=(kt + j == 0), stop=(kt + j == nkt - 1))
                return ins
            P.op("pe", pv, [vh.r[0], vh.r[1], pt.r[0]], [op_.r[0]])
            if kt + 2 == nkt:
                rc = rcp.next()
                ao = aos.next()
                P.op("dve", lambda e, rc=rc, op_=op_: e.reciprocal(out=rc[64:128, :], in_=op_[64:128, :]), [], [rc.r[0], op_.r[0]])
                P.op("dve", lambda e, rc=rc, op_=op_, ao=ao: e.tensor_tensor(out=ao[:, :], in0=op_[0:64, :], in1=rc[64:128, :], op=ALU.mult),
                     [rc.r[0]], [ao.r[0], op_.r[0]])
                dma(P, "sp", attn_dst(h, I), ao[:, :], [ao.r[0]], [ores])

        LA = 2
        pend = [emit_qk(gi) for gi in range(min(LA, len(groups)))]
        for gi in range(len(groups)):
            I_, kt_ = groups[gi]
            if kt_ == 0 and I_ in ins_at and nxt_prep:
                nxt_prep[ins_at[I_]]()
            if gi + LA < len(groups):
                pend.append(emit_qk(gi + LA))
            emit_pv(gi, pend.pop(0))


def moba_consts():
    onehot = np.zeros((32, S), np.float32)
    for n in range(32):
        onehot[n, n * 256:(n + 1) * 256] = 1.0
    k = np.arange(128)[:, None]
    q = np.arange(512)[None, :]
    tri = np.stack([((j * 128 + k) <= q).astype(np.float32) for j in range(4)], axis=1)
    ident = np.eye(128, dtype=np.float32)
    qb = (np.arange(64) // 2)[:, None]
    n = np.arange(32)[None, :]
    valid = (n < qb).astype(np.float32)
    own = (n == qb).astype(np.float32)
    cmask = (valid - 1.0) * 1e30
    g = np.stack([cmask.reshape(-1), valid.reshape(-1), own.reshape(-1)], axis=0).astype(np.float32)
    gconst = np.ascontiguousarray(np.broadcast_to(g[None], (128, 3, 2048))).astype(np.float32)
    return onehot, np.ascontiguousarray(tri), ident, gconst


DIL_D = (1, 4, 16)
DIL_MOFF = (0, 5, 13)
DIL_NM = 33


def phase_b_dil(cx, qT, kT, vv, dmask_d, attn_dst, ires, ores):
    P = cx.P
    Ks = [cx.sb([128, S], BF16, nres=2) for _ in range(3)]
    Vs = [cx.sb([128, 64, 128], BF16, nres=2) for _ in range(3)]
    mk = cx.sb([128, DIL_NM, 512], BF16, nres=DIL_NM)
    qts = Rot([cx.sb([128, 512], BF16, nres=2) for _ in range(6)])
    pts = Rot([cx.sb([128, 2, 512], BF16) for _ in range(4)])
    den = cx.sb([128, 512], F32)
    aos = Rot([cx.sb([64, 512], BF16) for _ in range(3)])
    sps = Rot([cx.ps([128, 1024]) for _ in range(2)])
    opg = [cx.ps([128, 512]) for _ in range(3)]
    cnt = 0
    for m in range(DIL_NM):
        dma(P, "sp", mk[:, m, :], dmask_d.ap()[:, m, :], [], [mk.r[m]])
    vview = vv.ap().rearrange("(kt p) n -> p kt n", p=128)
    cnt = 0
    for slot in range(2):
        if slot == 1:
            pass
        for g in range(3):
            hl = g * 2 + slot
            for q4 in range(4):
                sl = slice(q4 * 2048, (q4 + 1) * 2048)
                dma(P, "pool", Ks[g][0:64, sl], kT.ap()[hl * 64:(hl + 1) * 64, sl], [ires], [Ks[g].r[0]])
                dma(P, "pool", Vs[g][:, q4 * 16:(q4 + 1) * 16, 0:64], vview[:, q4 * 16:(q4 + 1) * 16, hl * 64:(hl + 1) * 64], [ires], [Vs[g].r[0]])
        if slot == 0:
            for b_ in Vs:
                P.op("pool", lambda e, b_=b_: e.memset(b_[:, :, 64:128], 1.0), [], [b_.r[1]])
            for b_ in Ks:
                P.op("pool", lambda e, b_=b_: e.memset(b_[64:128, :], 0.0), [], [b_.r[1]])
            for b_ in qts.tiles:
                P.op("pool", lambda e, b_=b_: e.memset(b_[64:128, :], 0.0), [], [b_.r[1]])

        def load_q(I, slot=slot):
            res = []
            for g in range(3):
                hl = g * 2 + slot
                qt = qts.next()
                dma(P, "pool", qt[0:64, :], qT.ap()[hl * 64:(hl + 1) * 64, I * 512:(I + 1) * 512], [ires], [qt.r[0]])
                res.append(qt)
            return res
        nxt_q = load_q(0)
        for I in range(NT):
            work = []
            cur_q = nxt_q
            if I + 1 < NT:
                nxt_q = load_q(I + 1)
            for g in range(3):
                hl = g * 2 + slot
                d = DIL_D[g]
                qt = cur_q[g]
                kts = [kt for kt in range(4 * I - d, 4 * I + 4) if kt >= 0]
                for c0 in range(0, len(kts), 2):
                    work.append((g, d, qt, kts, kts[c0:c0 + 2]))

            def emit_qk(w, I=I):
                g, d, qt, kts, grp = w
                n = len(grp)
                sp_ = sps.next()

                def smm(e, sp_=sp_, grp=grp, qt=qt, g=g):
                    for j, kt in enumerate(grp):
                        ins = e.matmul(sp_[:, j * 512:(j + 1) * 512], Ks[g][:, kt * 128:(kt + 1) * 128], qt[:, :], start=True, stop=True)
                    return ins
                P.op("pe", smm, [Ks[g].r[0], Ks[g].r[1], qt.r[0], qt.r[1]], [sp_.r[0]])
                pt = pts.next()
                P.op("act", lambda e, pt=pt, sp_=sp_, n=n: e.activation(out=pt[:, 0:n, :].rearrange("p a b -> p (a b)"),
                                                                        in_=sp_[:, 0:n * 512], func=AF.Exp), [], [pt.r[0], sp_.r[0]])
                m0 = DIL_MOFF[g] + (grp[0] - (4 * I - d))
                P.op("dve", lambda e, pt=pt, m0=m0, n=n: e.tensor_tensor(out=pt[:, 0:n, :], in0=pt[:, 0:n, :], in1=mk[:, m0:m0 + n, :], op=ALU.mult),
                     [mk.r[m0 + j] for j in range(n)], [pt.r[0]])
                return pt

            def emit_pv(w, pt):
                g, d, qt, kts, grp = w
                op_ = opg[g]

                def pv(e, op_=op_, pt=pt, grp=grp, kts=kts, g=g):
                    for j, kt in enumerate(grp):
                        ins = e.matmul(op_[:, :], Vs[g][:, kt, :], pt[:, j, :], start=(kt == kts[0]), stop=(kt == kts[-1]))
                    return ins
                P.op("pe", pv, [Vs[g].r[0], Vs[g].r[1], pt.r[0]], [op_.r[0]])

            LA = 2
            pend = [emit_qk(work[wi]) for wi in range(min(LA, len(work)))]
            for wi in range(len(work)):
                if wi + LA < len(work):
                    pend.append(emit_qk(work[wi + LA]))
                emit_pv(work[wi], pend.pop(0))
            P.op("act", lambda e: e.activation(out=den[64:128, :], in_=opg[0][64:128, :], func=AF.Copy), [], [den.r[0], opg[0].r[0]])
            P.op("dve", lambda e: e.tensor_tensor(out=den[64:128, :], in0=den[64:128, :], in1=opg[1][64:128, :], op=ALU.add), [], [den.r[0], opg[1].r[0]])
            P.op("dve", lambda e: e.tensor_tensor(out=den[64:128, :], in0=den[64:128, :], in1=opg[2][64:128, :], op=ALU.add), [], [den.r[0], opg[2].r[0]])
            P.op("dve", lambda e: e.reciprocal(out=den[64:128, :], in_=den[64:128, :]), [], [den.r[0]])
            for g in range(3):
                hl = g * 2 + slot
                ao = aos.next()
                P.op("dve", lambda e, ao=ao, g=g: e.tensor_tensor(out=ao[:, :], in0=opg[g][0:64, :], in1=den[64:128, :], op=ALU.mult),
                     [den.r[0]], [ao.r[0], opg[g].r[0]])
                dma(P, "sp", attn_dst(hl, I), ao[:, :], [ao.r[0]], [ores])


def dil_consts():
    k = np.arange(128)[:, None]
    q = np.arange(512)[None, :]
    tiles = []
    for g, d in enumerate(DIL_D):
        for delta in range(d + 4):
            diff = q - k + (d - delta) * 128
            tiles.append(((diff >= 0) & (diff <= 128 * d) & (diff % d == 0)).astype(np.float32))
    return np.ascontiguousarray(np.stack(tiles, axis=1))


RG = [[0, 1], [2, 3], [4, 5], [6, 7]]


def all_gather(nc, srcs, dsts):
    sems = []
    for _ in srcs:
        _UID[0] += 1
        sems.append(nc.alloc_semaphore("cc%d" % _UID[0]))
    with nc.Block() as block:
        @block.gpsimd
        def _(g):
            for src, dst, sem in zip(srcs, dsts, sems):
                g.collective_compute("AllGather", ALU.bypass, replica_groups=RG, ins=[src.ap()], outs=[dst.ap()]).then_inc(sem)
            for sem in sems:
                g.wait_ge(sem, 1)


def build_program():
    nc = bass.Bass("TRN2", target_bir_lowering=False)
    ei = lambda name, shape, dt=F32: nc.dram_tensor(name, list(shape), dt, kind="ExternalInput")
    xT = ei("xT", [D, S])
    xres = ei("xres", [D, HALF])
    wqkv = [ei("wqkv0", [D, 3 * 512]), ei("wqkv1", [D, 3 * 384])]
    wo = [ei("wo0", [1024, D]), ei("wo1", [768, D])]
    win = [ei("win0", [D, DFF]), ei("win1", [D, DFF])]
    wout = [ei("wout0", [DFF, D]), ei("wout1", [DFF, D])]
    lnp = [ei("lnp0", [128, 32]), ei("lnp1", [128, 32])]
    cos_d, sin_d, rot_d = ei("cos", [128, S]), ei("sin", [128, S]), ei("rot", [128, 128])
    onehot_d, tri_d, ident_d = ei("onehot", [32, S], BF16), ei("tri", [128, 4, 512], BF16), ei("ident", [128, 128], BF16)
    gconst_d, dmask_d = ei("gconst", [128, 3, 2048]), ei("dmask", [128, DIL_NM, 512], BF16)
    hsel_d = ei("hsel", [128, 2])
    yT = nc.dram_tensor("yT", [D, HALF], F32, kind="ExternalOutput")
    it = lambda name, shape, dt=BF16: nc.dram_tensor(name, list(shape), dt)
    nhs = (8, 6)
    qT = [it("qT%d" % l, [nhs[l] * 64, S]) for l in range(2)]
    kT = [it("kT%d" % l, [nhs[l] * 64, S]) for l in range(2)]
    vv = [it("v%d" % l, [S, nhs[l] * 64]) for l in range(2)]
    CH = 256
    a_in = [[it("ain%d_%d" % (l, j), [CH, HALF]) for j in range(2 * nhs[l] * 64 // CH)] for l in range(2)]
    a_g = [[it("ag%d_%d" % (l, j), [2 * CH, HALF]) for j in range(2 * nhs[l] * 64 // CH)] for l in range(2)]
    wp = [it("wp%d" % l, [9, 128, 8192]) for l in range(2)]
    res1 = it("res1", [D, HALF], F32)
    hg_in = [it("hgin%d" % j, [CH, HALF]) for j in range(D // CH)]
    hg_out = [it("hgout%d" % j, [2 * CH, HALF]) for j in range(D // CH)]

    xv = xT.ap().rearrange("(c p) s -> p c s", p=128)
    xrv = xres.ap().rearrange("(c p) s -> p c s", p=128)
    r1v = res1.ap().rearrange("(c p) s -> p c s", p=128)
    hgi = [h_.ap().rearrange("(c p) s -> p c s", p=128) for h_ in hg_in]
    hgo = [h_.ap().rearrange("(r c p) s -> r p c s", r=2, p=128) for h_ in hg_out]
    yv = yT.ap().rearrange("(c p) s -> p c s", p=128)
    tile_sl = lambda v: (lambda tt: v[:, :, tt * 512:(tt + 1) * 512])

    import os
    pool = SemPool(nc)
    nstage = int(os.environ.get("DBG_STAGES", "99"))
    stage = [0]

    def go():
        stage[0] += 1
        return stage[0] <= nstage
    for l in range(2):
        nh = nhs[l]
        nk = 2 * nh * 64 // 128
        if not go():
            break
        cx = Ctx(nc)
        with cx.st:
            hook = None
            if l == 0:
                hsrc, f32 = (lambda t: [(0, 8, xv[:, :, t * 512:(t + 1) * 512])]), True
            else:
                hsrc, f32 = (lambda t: [(2 * j, 2 * j + 2, hgo[j][t // 8][:, :, (t % 8) * 512:(t % 8 + 1) * 512]) for j in range(4)]), False
            phase_a(cx, hsrc, f32, wqkv[l], cos_d, sin_d, rot_d, qT[l], kT[l], vv[l], nh, Res(), Res(), hook=hook)
            cx.P.emit(pool)
        if not go():
            break
        cx = Ctx(nc)
        with cx.st:
            nhd = nh * 64
            def dst(h, I, l=l, nhd=nhd):
                row0 = (I // 8) * nhd + h * 64
                return a_in[l][row0 // CH].ap()[row0 % CH:row0 % CH + 64, (I % 8) * 512:(I % 8 + 1) * 512]
            if l == 0:
                stg = Rot([cx.sb([128, 8192], BF16, nres=4) for _ in range(2)])
                pc = [precast_weights(cx, wo[ll], win[ll], wout[ll], wp[ll], Res(), nk=2 * nhs[ll] * 64 // 128, stg=stg) for ll in range(2)]
                todo = [(ll, i) for ll in range(2) for i in range(9)]

                def mhook(h, pc=pc, todo=todo):
                    for (ll, i) in todo[h * 3:(h + 1) * 3]:
                        pc[ll](i)
                phase_b_moba(cx, qT[l], kT[l], vv[l], onehot_d, tri_d, ident_d, gconst_d, dst, nh, Res(), Res(), hook=mhook)
            else:
                phase_b_dil(cx, qT[l], kT[l], vv[l], dmask_d, dst, Res(), Res())
            cx.P.emit(pool)
        all_gather(nc, a_in[l], a_g[l])
        if not go():
            break
        cx = Ctx(nc)
        with cx.st:
            nhd = nh * 64
            kh = nk // 2

            def attn_src(tt, l=l, nhd=nhd, kh=kh, nk=nk):
                res = []
                for hh in range(2):
                    lst = []
                    for k in range(nk):
                        row0 = hh * nhd + (k % kh) * 128
                        rr = (k // kh) * CH + row0 % CH
                        lst.append(a_g[l][row0 // CH].ap()[rr:rr + 128, tt * 512:(tt + 1) * 512])
                    res.append(lst)
                return res
            if l == 0:
                phase_c(cx, attn_src, tile_sl(xrv), wp[l], lnp[l], tile_sl(r1v), (lambda tt: [(2 * j, 2 * j + 2, hgi[j][:, :, tt * 512:(tt + 1) * 512]) for j in range(4)]), Res(), Res(), Res(), Res(), nk=nk, hsel_d=hsel_d)
            else:
                phase_c(cx, attn_src, tile_sl(r1v), wp[l], lnp[l], tile_sl(yv), None, Res(), Res(), Res(), Res(), nk=nk, hsel_d=hsel_d)
            cx.P.emit(pool)
        if l == 0:
            all_gather(nc, hg_in, hg_out)
    return nc


def _lnp(g1, b1, g2, b2):
    return np.ascontiguousarray(np.concatenate([np.asarray(v, np.float32).reshape(8, 128).T for v in (g1, b1, g2, b2)], axis=1))


def make_in_maps(x, moba_w_qkv, moba_w_o, dil_w_qkv, dil_w_o, mlp_w_in, mlp_w_out, ln_mix_g, ln_mix_b, ln_mlp_g, ln_mlp_b):
    f = lambda a: np.ascontiguousarray(np.asarray(a, dtype=np.float32))
    cos128, sin128, R = rope_tables()
    onehot, tri, ident, gconst = moba_consts()
    dmask = dil_consts()
    wq0 = f(moba_w_qkv)[0]
    wq1 = f(dil_w_qkv)[0]
    shared = {"win0": f(mlp_w_in)[0], "win1": f(mlp_w_in)[1], "wout0": f(mlp_w_out)[0], "wout1": f(mlp_w_out)[1],
              "lnp0": _lnp(ln_mix_g[0], ln_mix_b[0], ln_mlp_g[0], ln_mlp_b[0]),
              "lnp1": _lnp(ln_mix_g[1], ln_mix_b[1], ln_mlp_g[1], ln_mlp_b[1]),
              "cos": cos128, "sin": sin128, "rot": R, "onehot": onehot.astype(ml_dtypes.bfloat16), "tri": tri.astype(ml_dtypes.bfloat16),
              "ident": ident.astype(ml_dtypes.bfloat16), "gconst": gconst, "dmask": dmask.astype(ml_dtypes.bfloat16),
              "wo0": f(moba_w_o)[0]}
    heads1 = [[g * 4 + 2 * r + sl for g in range(3) for sl in range(2)] for r in range(2)]
    rows1 = np.concatenate([np.arange(h * 64, (h + 1) * 64) for r in range(2) for h in heads1[r]])
    shared["wo1"] = np.ascontiguousarray(f(dil_w_o)[0][rows1])
    maps = []
    xf = f(x)
    for c in range(8):
        b, r = c // 2, c % 2
        xT = np.ascontiguousarray(xf[b].T)
        m = dict(shared)
        m["xT"] = xT
        hs = np.zeros((128, 2), np.float32)
        hs[:, r] = 1.0
        m["hsel"] = hs
        m["xres"] = np.ascontiguousarray(xT[:, r * HALF:(r + 1) * HALF])
        cols0 = np.concatenate([np.arange(part * 1024 + (8 * r) * 64, part * 1024 + (8 * r + 8) * 64) for part in range(3)])
        m["wqkv0"] = np.ascontiguousarray(wq0[:, cols0])
        cols1 = np.concatenate([np.arange(part * 768 + h * 64, part * 768 + (h + 1) * 64) for part in range(3) for h in heads1[r]])
        m["wqkv1"] = np.ascontiguousarray(wq1[:, cols1])
        maps.append(m)
    return maps


def kernel(**inputs):
    nc = build_program()
    maps = make_in_maps(**inputs)
    res = run_bass_kernel_spmd(nc, maps, core_ids=list(range(8)))
    out = np.empty((B, S, D), np.float32)
    for c in range(8):
        b, r = c // 2, c % 2
        out[b, r * HALF:(r + 1) * HALF, :] = np.asarray(res.results[c]["yT"]).T
    return out
```

```python
import numpy as np
import ml_dtypes
import concourse.bass as bass
import concourse.mybir as mybir
from concourse.bass_utils import run_bass_kernel_spmd

F32 = mybir.dt.float32
BF16 = mybir.dt.bfloat16
AF = mybir.ActivationFunctionType
ALU = mybir.AluOpType
AX = mybir.AxisListType

D = 1024
S = 8192
B = 4
DFF = 4096
DH = 64
NT = S // 512
HALF = S // 2
LN_EPS = 1e-5
ALPHA = (2.0 * 2) ** 0.25
NEG = -30000.0
DIL = ((128, 1), (512, 4), (2048, 16))


class Res:
    __slots__ = ("w", "rc", "rd")

    def __init__(self):
        self.w = None
        self.rc = {}
        self.rd = []


class Op:
    __slots__ = ("eng", "fn", "deps", "sig", "val", "dma", "sem")


class Prog:
    ENGS = ("pe", "act", "dve", "pool", "sp")
    NDSEM = 12

    def __init__(self, nc):
        self.nc = nc
        self.ops = {e: [] for e in self.ENGS}
        self.dma_hist = {e: [None] * self.NDSEM for e in ("sp", "pool", "act")}
        self.dma_cnt = {e: 0 for e in ("sp", "pool", "act")}
        self.dma_val = {e: [0] * self.NDSEM for e in ("sp", "pool", "act")}

    def op(self, eng, fn, reads=(), writes=(), dma=False):
        o = Op()
        o.eng, o.fn, o.dma, o.sig, o.val, o.sem = eng, fn, dma, False, 0, None
        deps = {}
        for r in reads:
            if r.w is not None:
                deps[id(r.w)] = r.w
        for w in writes:
            if w.w is not None:
                deps[id(w.w)] = w.w
            for x in w.rc.values():
                deps[id(x)] = x
            for x in w.rd:
                deps[id(x)] = x
        if dma:
            k = self.dma_cnt[eng] % self.NDSEM
            self.dma_cnt[eng] += 1
            prev = self.dma_hist[eng][k]
            if prev is not None:
                deps[id(prev)] = prev
            self.dma_hist[eng][k] = o
            self.dma_val[eng][k] += 16
            o.sem = (eng, k)
            o.val = self.dma_val[eng][k]
        o.deps = []
        for d in deps.values():
            if d is o:
                continue
            if (not d.dma) and (not dma) and d.eng == "pe" and eng == "pe":
                continue
            if not d.dma:
                d.sig = True
            o.deps.append(d)
        for r in reads:
            if dma:
                r.rd.append(o)
            else:
                r.rc[eng] = o
        for w in writes:
            w.w = o
            w.rc = {}
            w.rd = []
        self.ops[eng].append(o)
        return o

    def emit(self, pool=None):
        nc = self.nc
        if pool is None:
            pool = SemPool(nc)
        for e in self.ENGS:
            c = pool.ebase[e]
            for o in self.ops[e]:
                if o.sig and not o.dma:
                    c += 1
                    o.val = c
            pool.ebase[e] = c
        for e in self.ENGS:
            for o in self.ops[e]:
                if o.dma:
                    o.val += pool.dbase[o.sem]
        esem, dsem = pool.esem, pool.dsem
        with nc.Block() as block:
            def run(e, eng):
                waited = {}
                for o in self.ops[e]:
                    for d in o.deps:
                        s = dsem[d.sem] if d.dma else esem[d.eng]
                        key = d.sem if d.dma else d.eng
                        if waited.get(key, 0) < d.val:
                            eng.wait_ge(s, d.val)
                            waited[key] = d.val
                    ins = o.fn(eng)
                    if o.dma:
                        ins.then_inc(dsem[o.sem], 16)
                    elif o.sig:
                        ins.then_inc(esem[e], 1)
                if e in ("sp", "pool") and self.dma_cnt[e]:
                    for k in range(min(self.NDSEM, self.dma_cnt[e])):
                        eng.wait_ge(dsem[(e, k)], pool.dbase[(e, k)] + self.dma_val[e][k])

            @block.tensor
            def _(t):
                run("pe", t)

            @block.scalar
            def _(s):
                run("act", s)

            @block.vector
            def _(v):
                run("dve", v)

            @block.gpsimd
            def _(g):
                run("pool", g)

            @block.sync
            def _(sy):
                run("sp", sy)
        for e in ("sp", "pool"):
            for k in range(self.NDSEM):
                pool.dbase[(e, k)] += self.dma_val[e][k]


class SemPool:
    def __init__(self, nc):
        _UID[0] += 1
        u = "_%d" % _UID[0]
        self.nc = nc
        self.esem = {e: nc.alloc_semaphore("es_" + e + u) for e in Prog.ENGS}
        self.dsem = {(q, k): nc.alloc_semaphore("ds_%s%d%s" % (q, k, u)) for q in ("sp", "pool") for k in range(Prog.NDSEM)}
        self.ebase = {e: 0 for e in Prog.ENGS}
        self.dbase = {key: 0 for key in self.dsem}


_UID = [0]


class Tile:
    def __init__(self, t, nres=1):
        self.t = t
        self.r = [Res() for _ in range(nres)]

    def __getitem__(self, k):
        return self.t[k]


class Ctx:
    def __init__(self, nc):
        import contextlib
        self.nc = nc
        self.P = Prog(nc)
        self.st = contextlib.ExitStack()
        self.n = 0

    def sb(self, shape, dt, nres=1):
        _UID[0] += 1
        return Tile(self.st.enter_context(self.nc.sbuf_tensor("sb%d" % _UID[0], list(shape), dt)), nres)

    def ps(self, shape, dt=F32, nres=1):
        _UID[0] += 1
        return Tile(self.st.enter_context(self.nc.psum_tensor("ps%d" % _UID[0], list(shape), dt)), nres)

    def dram(self, name, shape, dt, kind="Internal"):
        return self.nc.dram_tensor(name, list(shape), dt, kind=kind)


class Rot:
    def __init__(self, tiles):
        self.tiles = tiles
        self.i = 0

    def next(self):
        t = self.tiles[self.i % len(self.tiles)]
        self.i += 1
        return t


def dma(P, q, out, in_, reads, writes):
    def f(e):
        return e.dma_start(out=out(e) if callable(out) else out, in_=in_(e) if callable(in_) else in_)
    return P.op(q, f, reads=reads, writes=writes, dma=True)


def phase_a(cx, hsrc, h_is_f32, wqkv, cos_d, sin_d, rotm_d, qT, kT, vv, nh, hres, ores, hook=None):
    nc, P = cx.nc, cx.P
    npair = nh // 2
    ncol = nh * DH
    w_sb = cx.sb([128, 8, 3 * ncol], BF16, nres=8)
    rot_sb = cx.sb([128, 128], BF16)
    wv = wqkv.ap().rearrange("(c p) n -> p c n", p=128)
    for c in range(8):
        dma(P, "pool", w_sb[:, c, :], wv[:, c, :], [], [w_sb.r[c]])
    dma(P, "pool", rot_sb[:, :], rotm_d.ap(), [], [rot_sb.r[0]])
    hts = Rot([cx.sb([128, 8, 512], BF16) for _ in range(2)])
    cs = Rot([cx.sb([128, 2, 512], F32, nres=2) for _ in range(2)])
    tbs = Rot([cx.sb([128, 512], BF16) for _ in range(3)])
    tmp1 = Rot([cx.sb([128, 512], F32) for _ in range(2)])
    tmp2 = Rot([cx.sb([128, 512], F32) for _ in range(2)])
    outs = Rot([cx.sb([128, 512], BF16) for _ in range(3)])
    vouts = Rot([cx.sb([128, ncol], BF16) for _ in range(2)])
    psA = Rot([cx.ps([128, 512]) for _ in range(3)])
    psR = Rot([cx.ps([128, 512]) for _ in range(2)])
    psV = Rot([cx.ps([128, ncol]) for _ in range(2)])
    cosv = cos_d.ap()
    sinv = sin_d.ap()
    def load_h(t):
        ht = hts.next()
        for (c0, c1, hap) in hsrc(t):
            dma(P, "pool", ht[:, c0:c1, :], hap, [hres], [ht.r[0]])
        ct = cs.next()
        dma(P, "pool", ct[:, 0, :], cosv[:, t * 512:(t + 1) * 512], [], [ct.r[0]])
        dma(P, "pool", ct[:, 1, :], sinv[:, t * 512:(t + 1) * 512], [], [ct.r[1]])
        return ht, ct
    nxt_h = load_h(0)
    for t in range(NT):
        ht, ct = nxt_h
        if t + 1 < NT:
            nxt_h = load_h(t + 1)
        if hook is not None:
            hook(t)
        pend = None
        jobs = [(which, p) for which in range(2) for p in range(npair)]
        for (which, p) in jobs + [(None, None)]:
            cur = None
            if (which, p) is not None and p is not None:
                pass
            if which is not None:
                dst = qT if which == 0 else kT
                col0 = which * ncol + p * 128
                pa = psA.next()

                def mm(e, pa=pa, ht=ht, col0=col0):
                    for c in range(8):
                        ins = e.matmul(pa[:, :], w_sb[:, c, col0:col0 + 128], ht[:, c, :], start=(c == 0), stop=(c == 7))
                    return ins
                P.op("pe", mm, list(w_sb.r) + [ht.r[0]], [pa.r[0]])
                tb = tbs.next()
                P.op("act", lambda e, tb=tb, pa=pa: e.activation(out=tb[:, :], in_=pa[:, :], func=AF.Copy),
                     [], [tb.r[0], pa.r[0]])
                cur = (which, p, dst, pa, tb)
            if pend is not None:
                pw, pp, pdst, ppa, ptb = pend
                pr = psR.next()
                P.op("pe", lambda e, pr=pr, tb=ptb: e.matmul(pr[:, :], rot_sb[:, :], tb[:, :], start=True, stop=True),
                     [rot_sb.r[0], ptb.r[0]], [pr.r[0]])
                t1 = tmp1.next()
                t2 = tmp2.next()
                sc = 0.125 if pw == 0 else 1.0
                P.op("dve", lambda e, t1=t1, pa=ppa, ct=ct, sc=sc: e.scalar_tensor_tensor(
                    out=t1[:, :], in0=pa[:, :], scalar=sc, in1=ct[:, 0, :], op0=ALU.mult, op1=ALU.mult),
                    [ct.r[0]], [t1.r[0], ppa.r[0]])
                P.op("dve", lambda e, t2=t2, pr=pr, ct=ct, sc=sc: e.scalar_tensor_tensor(
                    out=t2[:, :], in0=pr[:, :], scalar=sc, in1=ct[:, 1, :], op0=ALU.mult, op1=ALU.mult),
                    [ct.r[1]], [t2.r[0], pr.r[0]])
                ob = outs.next()
                P.op("pool", lambda e, ob=ob, t1=t1, t2=t2: e.tensor_tensor(out=ob[:, :], in0=t1[:, :], in1=t2[:, :], op=ALU.add),
                     [t1.r[0], t2.r[0]], [ob.r[0]])
                dma(P, "sp", pdst.ap()[pp * 128:(pp + 1) * 128, t * 512:(t + 1) * 512], ob[:, :], [ob.r[0]], [ores])
            pend = cur
        for s4 in range(4):
            pv = psV.next()

            def mmv(e, pv=pv, ht=ht, s4=s4):
                for c in range(8):
                    ins = e.matmul(pv[:, :], ht[:, c, s4 * 128:(s4 + 1) * 128], w_sb[:, c, 2 * ncol:3 * ncol],
                                   start=(c == 0), stop=(c == 7))
                return ins
            P.op("pe", mmv, list(w_sb.r) + [ht.r[0]], [pv.r[0]])
            vo = vouts.next()
            P.op("act", lambda e, vo=vo, pv=pv: e.activation(out=vo[:, :], in_=pv[:, :], func=AF.Copy),
                 [], [vo.r[0], pv.r[0]])
            r0 = t * 512 + s4 * 128
            dma(P, "sp", vv.ap()[r0:r0 + 128, :], vo[:, :], [vo.r[0]], [ores])


def rope_tables():
    inv = (1.0 / (10000.0 ** (np.arange(0, DH, 2, dtype=np.float32) / DH))).astype(np.float32)
    ang = (np.arange(S, dtype=np.float32)[:, None] * inv[None, :]).astype(np.float32)
    cos = np.cos(ang).astype(np.float32).T
    sin = np.sin(ang).astype(np.float32).T
    cos128 = np.ascontiguousarray(np.tile(cos, (4, 1)))
    sin128 = np.ascontiguousarray(np.tile(sin, (4, 1)))
    R = np.zeros((128, 128), np.float32)
    for hh in range(2):
        for j in range(32):
            R[hh * 64 + 32 + j, hh * 64 + j] = -1.0
            R[hh * 64 + j, hh * 64 + 32 + j] = 1.0
    return cos128, sin128, R


def precast_weights(cx, wo_d, win_d, wout_d, wp, wres, nk=8, stg=None):
    P = cx.P
    if stg is None:
        stg = Rot([cx.sb([128, 8192], BF16, nres=4) for _ in range(2)])
    wov = wo_d.ap().rearrange("(k p) d -> p k d", p=128)
    winv = win_d.ap().rearrange("(k p) f -> p k f", p=128)
    woutv = wout_d.ap().rearrange("(f p) d -> p f d", p=128)

    def piece(i):
        st = stg.next()
        if i == 0:
            src, view = wov, st.t[:, 0:nk * 1024].rearrange("p (k d) -> p k d", k=nk)
        elif i <= 4:
            j = i - 1
            src, view = winv[:, :, j * 1024:(j + 1) * 1024], st.t[:, :].rearrange("p (k f) -> p k f", k=8)
        else:
            j = i - 5
            src, view = woutv[:, :, j * 256:(j + 1) * 256], st.t[:, :].rearrange("p (f d) -> p f d", f=32)
        nsp = 2
        n0 = view.shape[1] // nsp
        for q in range(nsp):
            dma(P, "pool", view[:, q * n0:(q + 1) * n0, :], src[:, q * n0:(q + 1) * n0, :], [], [st.r[q]])
        dma(P, "sp", wp.ap()[i], st[:, :], list(st.r), [wres])
    return piece


def phase_c(cx, attn_src, res_src, wp, lnp_d, out_f32_dst, out_bf16_dst, ares, rres, wres, ores, nk=8, hsel_d=None):
    P = cx.P
    ones = cx.sb([128, 128], BF16)
    P.op("pool", lambda e: e.memset(ones[:, :], 1.0), [], [ones.r[0]])
    lnp = cx.sb([128, 32], F32)
    dma(P, "sp", lnp[:, :], lnp_d.ap(), [], [lnp.r[0]])
    hsel = cx.sb([128, 2], F32)
    if hsel_d is not None:
        dma(P, "sp", hsel[:, :], hsel_d.ap(), [], [hsel.r[0]])
    wsl = Rot([cx.sb([128, 8192], BF16) for _ in range(4)])
    ats = Rot([cx.sb([128, 8, 512], BF16, nres=8) for _ in range(1)])
    ob16 = cx.sb([128, 8, 512], BF16, nres=8)
    Rs = Rot([cx.sb([128, 8, 512], F32, nres=8) for _ in range(2)])
    hb = cx.sb([128, 8, 512], BF16, nres=8)
    u = cx.sb([128, 32, 512], BF16, nres=32)
    zb = cx.sb([128, 8, 512], BF16, nres=8)
    zq = cx.sb([128, 8, 512], BF16, nres=8)
    mt = cx.sb([128, 512], F32)
    msq = cx.sb([128, 512], F32)
    rstd = cx.sb([128, 512], F32)
    rl = Rot([cx.sb([128, 512], F32) for _ in range(2)])
    acc = Rot([cx.ps([128, 512]) for _ in range(4)])
    st1 = cx.ps([128, 512])
    st2 = cx.ps([128, 512])
    wpv = wp.ap()

    def load_piece(i):
        w = wsl.next()
        for q in range(2):
            dma(P, "pool", w[:, q * 4096:(q + 1) * 4096], wpv[i][:, q * 4096:(q + 1) * 4096], [wres], [w.r[0]])
        return w

    def ln_pre(R):
        allr = list(R.r)
        P.op("act", lambda e: e.activation(out=zb[:, :, :], in_=R[:, :, :], func=AF.Copy), allr, list(zb.r))
        P.op("act", lambda e: e.activation(out=zq[:, :, :], in_=R[:, :, :], func=AF.Square), allr, list(zq.r))

    def ln_stats():
        def s1(e):
            for c in range(8):
                ins = e.matmul(st1[:, :], ones[:, :], zb[:, c, :], start=(c == 0), stop=(c == 7))
            return ins

        def s2(e):
            for c in range(8):
                ins = e.matmul(st2[:, :], ones[:, :], zq[:, c, :], start=(c == 0), stop=(c == 7))
            return ins
        P.op("pe", s1, [ones.r[0]] + list(zb.r), [st1.r[0]])
        P.op("pe", s2, [ones.r[0]] + list(zq.r), [st2.r[0]])

    def ln_post(R, gi, bi, bft, f32_dst, bf_dst):
        allr = list(R.r)
        P.op("act", lambda e: e.activation(out=mt[:, :], in_=st1[:, :], func=AF.Copy, scale=1.0 / D), [], [mt.r[0], st1.r[0]])
        P.op("dve", lambda e: e.tensor_tensor(out=msq[:, :], in0=mt[:, :], in1=mt[:, :], op=ALU.mult), [mt.r[0]], [msq.r[0]])
        P.op("dve", lambda e: e.scalar_tensor_tensor(out=rstd[:, :], in0=st2[:, :], scalar=1.0 / D, in1=msq[:, :],
                                                      op0=ALU.mult, op1=ALU.subtract), [msq.r[0]], [rstd.r[0], st2.r[0]])
        P.op("dve", lambda e: e.tensor_scalar_add(out=rstd[:, :], in0=rstd[:, :], scalar1=LN_EPS), [], [rstd.r[0]])
        P.op("act", lambda e: e.activation(out=rstd[:, :], in_=rstd[:, :], func=AF.Sqrt), [], [rstd.r[0]])
        P.op("dve", lambda e: e.reciprocal(out=rstd[:, :], in_=rstd[:, :]), [], [rstd.r[0]])
        for c in range(8):
            P.op("dve", lambda e, c=c: e.tensor_tensor(out=R[:, c, :], in0=R[:, c, :], in1=mt[:, :], op=ALU.subtract),
                 [mt.r[0]], [R.r[c]])
            P.op("dve", lambda e, c=c: e.tensor_tensor(out=R[:, c, :], in0=R[:, c, :], in1=rstd[:, :], op=ALU.mult),
                 [rstd.r[0]], [R.r[c]])
            if bft is not None:
                P.op("act", lambda e, c=c: e.activation(out=bft[:, c, :], in_=R[:, c, :], func=AF.Identity,
                                                         scale=lnp[:, gi * 8 + c:gi * 8 + c + 1], bias=lnp[:, bi * 8 + c:bi * 8 + c + 1]),
                     [lnp.r[0], R.r[c]], [bft.r[c]])
            P.op("act", lambda e, c=c: e.activation(out=R[:, c, :], in_=R[:, c, :], func=AF.Identity,
                                                     scale=lnp[:, gi * 8 + c:gi * 8 + c + 1], bias=lnp[:, bi * 8 + c:bi * 8 + c + 1]),
                 [lnp.r[0]], [R.r[c]])
        if f32_dst is not None:
            dma(P, "sp", f32_dst, R[:, :, :], allr, [ores])
        if bf_dst is not None:
            for (c0, c1, dap) in bf_dst:
                dma(P, "sp", dap, bft[:, c0:c1, :], list(bft.r[c0:c1]), [ores])

    ntile = HALF // 512
    Rt = {}

    def emit_wo(tt):
        at = ats.next()
        R = Rs.next()
        Rt[tt] = R
        if hsel_d is None:
            dma(P, "sp", at[:, 0:nk, :], attn_src(tt), [ares], list(at.r))
        else:
            s0, s1 = attn_src(tt)
            for k in range(nk):
                dma(P, "sp", zb[:, k, :], s0[k], [ares], [zb.r[k]])
                dma(P, "sp", zq[:, k, :], s1[k], [ares], [zq.r[k]])
            P.op("act", lambda e: e.activation(out=zq[:, 0:nk, :], in_=zq[:, 0:nk, :], func=AF.Identity, scale=hsel[:, 1:2]), [hsel.r[0]], list(zq.r))
            P.op("dve", lambda e, at=at: e.scalar_tensor_tensor(out=at[:, 0:nk, :], in0=zb[:, 0:nk, :], scalar=hsel[:, 0:1], in1=zq[:, 0:nk, :],
                                                             op0=ALU.mult, op1=ALU.add), [hsel.r[0]] + list(zb.r) + list(zq.r), list(at.r))
        dma(P, "sp", R[:, :, :], res_src(tt), [rres], list(R.r))
        w = load_piece(0)
        wv = w.t[:, 0:nk * 1024].rearrange("p (k d) -> p k d", k=nk)
        for c in range(8):
            pa = acc.next()

            def mmo(e, pa=pa, c=c, wv=wv, at=at):
                for k in range(nk):
                    ins = e.matmul(pa[:, :], wv[:, k, c * 128:(c + 1) * 128], at[:, k, :], start=(k == 0), stop=(k == nk - 1))
                return ins
            P.op("pe", mmo, [w.r[0]] + list(at.r), [pa.r[0]])
            P.op("dve", lambda e, pa=pa, c=c, R=R: e.scalar_tensor_tensor(out=R[:, c, :], in0=R[:, c, :], scalar=ALPHA, in1=pa[:, :],
                                                                      op0=ALU.mult, op1=ALU.add), [], [R.r[c], pa.r[0]])
        ln_pre(R)

    def emit_up(tt, j):
        w = load_piece(1 + j)
        wv = w.t[:, :].rearrange("p (k f) -> p k f", k=8)
        for f in range(8):
            pa = acc.next()

            def mmu(e, pa=pa, f=f, wv=wv):
                for k in range(8):
                    ins = e.matmul(pa[:, :], wv[:, k, f * 128:(f + 1) * 128], hb[:, k, :], start=(k == 0), stop=(k == 7))
                return ins
            P.op("pe", mmu, [w.r[0]] + list(hb.r), [pa.r[0]])
            r_ = rl.next()
            P.op("act", lambda e, pa=pa, r_=r_: e.activation(out=r_[:, :], in_=pa[:, :], func=AF.Relu), [], [r_.r[0], pa.r[0]])
            fi = j * 8 + f
            P.op("dve", lambda e, pa=pa, r_=r_, fi=fi: e.tensor_tensor(out=u[:, fi, :], in0=r_[:, :], in1=pa[:, :], op=ALU.mult),
                 [r_.r[0]], [u.r[fi], pa.r[0]])

    def emit_down(tt, j):
        R = Rt[tt]
        w = load_piece(5 + j)
        wv = w.t[:, :].rearrange("p (f d) -> p f d", f=32)
        for cc in range(2):
            c = 2 * j + cc
            pa = acc.next()

            def mmd(e, pa=pa, cc=cc, wv=wv):
                for f in range(32):
                    ins = e.matmul(pa[:, :], wv[:, f, cc * 128:(cc + 1) * 128], u[:, f, :], start=(f == 0), stop=(f == 31))
                return ins
            P.op("pe", mmd, [w.r[0]] + list(u.r), [pa.r[0]])
            P.op("dve", lambda e, pa=pa, c=c, R=R: e.scalar_tensor_tensor(out=R[:, c, :], in0=R[:, c, :], scalar=ALPHA, in1=pa[:, :],
                                                                      op0=ALU.mult, op1=ALU.add), [], [R.r[c], pa.r[0]])

    def ln1_finish(tt):
        ln_stats()
        ln_post(Rt[tt], 0, 1, hb, None, None)

    def ln2_stats_post(tt):
        ln_stats()
        bf = out_bf16_dst(tt) if out_bf16_dst is not None else None
        ln_post(Rt[tt], 2, 3, ob16 if bf is not None else None, out_f32_dst(tt), bf)

    emit_wo(0)
    ln1_finish(0)
    for j in range(4):
        emit_up(0, j)
    for tt in range(ntile):
        nxt = tt + 1 < ntile
        if nxt:
            emit_wo(tt + 1)
        emit_down(tt, 0)
        emit_down(tt, 1)
        if nxt:
            ln1_finish(tt + 1)
        emit_down(tt, 2)
        emit_down(tt, 3)
        ln_pre(Rt[tt])
        if nxt:
            emit_up(tt + 1, 0)
        ln2_stats_post(tt)
        if nxt:
            for j in range(1, 4):
                emit_up(tt + 1, j)


def phase_b_moba(cx, qT, kT, vv, onehot_d, tri_d, ident_d, gconst_d, attn_dst, nh, ires, ores, hook=None):
    P = cx.P
    NQT = S // 128
    Qa = Rot([cx.sb([96, S], BF16, nres=2) for _ in range(2)])
    Ka = Rot([cx.sb([96, S], BF16, nres=2) for _ in range(2)])
    Vh = Rot([cx.sb([128, 64, 128], BF16, nres=2) for _ in range(2)])
    tri = cx.sb([128, 4, 512], BF16)
    ident = cx.sb([128, 128], BF16)
    gcon = cx.sb([128, 3, 2048], F32)
    kmf = cx.sb([64, 32], F32)
    kmb = cx.sb([96, 32], BF16, nres=2)
    gs = cx.sb([128, 512], F32)
    m8 = cx.sb([128, 16, 8], F32)
    sel = cx.sb([128, 512], F32)
    bpad = cx.sb([128, 16, 96], BF16)
    pts = Rot([cx.sb([128, 2, 512], BF16) for _ in range(4)])
    rcp = Rot([cx.sb([128, 512], F32) for _ in range(2)])
    aos = Rot([cx.sb([64, 512], BF16) for _ in range(2)])
    sps = Rot([cx.ps([128, 1024]) for _ in range(3)])
    ops_ = Rot([cx.ps([128, 512]) for _ in range(2)])
    dma(P, "sp", tri[:, :, :], tri_d.ap(), [], [tri.r[0]])
    dma(P, "sp", ident[:, :], ident_d.ap(), [], [ident.r[0]])
    dma(P, "sp", gcon[:, :, :], gconst_d.ap(), [], [gcon.r[0]])
    for b_ in Ka.tiles:
        dma(P, "sp", b_[64:96, :], onehot_d.ap(), [], [b_.r[1]])
    vview = vv.ap().rearrange("(kt p) n -> p kt n", p=128)
    def load_head(h):
        qa, ka, vh = Qa.next(), Ka.next(), Vh.next()
        for q4 in range(4):
            sl = slice(q4 * 2048, (q4 + 1) * 2048)
            dma(P, "pool", qa[0:64, sl], qT.ap()[h * 64:(h + 1) * 64, sl], [ires], [qa.r[0]])
            dma(P, "pool", ka[0:64, sl], kT.ap()[h * 64:(h + 1) * 64, sl], [ires], [ka.r[0]])
        for q4 in range(4):
            dma(P, "pool", vh[:, q4 * 16:(q4 + 1) * 16, 0:64], vview[:, q4 * 16:(q4 + 1) * 16, h * 64:(h + 1) * 64], [ires], [vh.r[0]])
        return qa, ka, vh
    nxt_head = load_head(0)
    for b_ in Vh.tiles:
        P.op("pool", lambda e, b_=b_: e.memset(b_[:, :, 64:128], 1.0), [], [b_.r[1]])
    P.op("pool", lambda e: e.memset(bpad[:, :, :], 0.0), [], [bpad.r[0]])
    P.op("pool", lambda e: e.memset(kmb[64:96, :], 0.0), [], [kmb.r[1]])
    for b_ in Qa.tiles:
        P.op("pool", lambda e, b_=b_: e.memset(b_[64:96, :], 0.0), [], [b_.r[1]])
    for h in range(nh):
        qa, ka, vh = nxt_head
        if h + 1 < nh:
            nxt_head = load_head(h + 1)
        if hook is not None:
            hook(h)
        def make_prep(qa, ka):
            st = {}

            def p_km():
                P.op("dve", lambda e, ka=ka: e.tensor_reduce(out=kmf[:, :], in_=ka[0:64, :].rearrange("p (n k) -> p n k", k=256),
                                                              axis=AX.X, op=ALU.add), [ka.r[0]], [kmf.r[0]])
                P.op("dve", lambda e: e.tensor_scalar_mul(out=kmb[0:64, :], in0=kmf[:, :], scalar1=1.0 / 256), [kmf.r[0]], [kmb.r[0]])

            def s12(g):
                gp = sps.next()

                def gmm(e, gp=gp, g=g, qa=qa):
                    for j in range(16):
                        qt = g * 16 + j
                        ins = e.matmul(gp[:, j * 32:(j + 1) * 32], qa[0:96, qt * 128:(qt + 1) * 128], kmb[:, :], start=True, stop=True)
                    return ins
                P.op("pe", gmm, [qa.r[0], qa.r[1], kmb.r[0], kmb.r[1]], [gp.r[0]])
                cs_ = slice(g * 512, (g + 1) * 512)
                P.op("dve", lambda e, gp=gp, cs_=cs_: e.tensor_tensor(out=gs[:, :], in0=gp[:, 0:512], in1=gcon[:, 0, cs_], op=ALU.add),
                     [gcon.r[0]], [gs.r[0], gp.r[0]])

                def mx(e):
                    for j in range(16):
                        ins = e.max(out=m8[:, j, :], in_=gs[:, j * 32:(j + 1) * 32])
                    return ins
                P.op("dve", mx, [gs.r[0]], [m8.r[0]])

                def ge(e):
                    for j in range(16):
                        ins = e.tensor_scalar(out=sel[:, j * 32:(j + 1) * 32], in0=gs[:, j * 32:(j + 1) * 32],
                                              scalar1=m8[:, j, 2:3], scalar2=None, op0=ALU.is_ge)
                    return ins
                P.op("dve", ge, [gs.r[0], m8.r[0]], [sel.r[0]])
                P.op("dve", lambda e, cs_=cs_: e.tensor_tensor(out=sel[:, :], in0=sel[:, :], in1=gcon[:, 1, cs_], op=ALU.mult),
                     [gcon.r[0]], [sel.r[0]])
                P.op("dve", lambda e, cs_=cs_: e.tensor_tensor(out=sel[:, :], in0=sel[:, :], in1=gcon[:, 2, cs_], op=ALU.add),
                     [gcon.r[0]], [sel.r[0]])
                P.op("dve", lambda e: e.tensor_scalar(out=bpad[:, :, 64:96], in0=sel[:, :].rearrange("p (j n) -> p j n", n=32),
                                                       scalar1=1.0, scalar2=-NEG, op0=ALU.subtract, op1=ALU.mult),
                     [sel.r[0]], [bpad.r[0]])

            def s34(g):
                for j4 in range(4):
                    tp = sps.next()

                    def tmm(e, tp=tp, j4=j4):
                        for jj in range(4):
                            ins = e.matmul(tp[0:96, jj * 128:(jj + 1) * 128], bpad[:, j4 * 4 + jj, :], ident[:, :], start=True, stop=True)
                        return ins
                    P.op("pe", tmm, [bpad.r[0], ident.r[0]], [tp.r[0]])
                    q0 = (g * 16 + j4 * 4) * 128
                    P.op("act", lambda e, tp=tp, q0=q0, qa=qa: e.activation(out=qa[64:96, q0:q0 + 512], in_=tp[64:96, 0:512], func=AF.Copy),
                         [], [qa.r[1], tp.r[0]])
            stages = [p_km, lambda: s12(0)]
            for g in range(1, 4):
                stages.append(lambda g=g: (s34(g - 1), s12(g)))
            stages.append(lambda: s34(3))
            return stages

        if h == 0:
            for pc_ in make_prep(qa, ka):
                pc_()
        nxt_prep = make_prep(nxt_head[0], nxt_head[1]) if h + 1 < nh else []
        ins_at = {4: 0, 6: 1, 8: 2, 10: 3, 12: 4, 14: 5}
        groups = [(I, kt) for I in range(NT) for kt in range(0, 4 * (I + 1), 2)]
        opt = {}

        def emit_qk(gi, qa=qa, ka=ka):
            I, kt = groups[gi]
            qsl = slice(I * 512, (I + 1) * 512)
            sp_ = sps.next()

            def smm(e, sp_=sp_, kt=kt, qsl=qsl):
                for j in range(2):
                    ins = e.matmul(sp_[:, j * 512:(j + 1) * 512], ka[0:96, (kt + j) * 128:(kt + j + 1) * 128], qa[0:96, qsl],
                                   start=True, stop=True)
                return ins
            P.op("pe", smm, [qa.r[0], qa.r[1], ka.r[0], ka.r[1]], [sp_.r[0]])
            pt = pts.next()
            P.op("act", lambda e, pt=pt, sp_=sp_: e.activation(out=pt[:, :, :].rearrange("p a b -> p (a b)"), in_=sp_[:, :], func=AF.Exp),
                 [], [pt.r[0], sp_.r[0]])
            if kt >= 4 * I:
                d0 = kt - 4 * I
                P.op("dve", lambda e, pt=pt, d0=d0: e.tensor_tensor(out=pt[:, :, :], in0=pt[:, :, :], in1=tri[:, d0:d0 + 2, :], op=ALU.mult),
                     [tri.r[0]], [pt.r[0]])
            return pt

        def emit_pv(gi, pt, vh=vh, h=h):
            I, kt = groups[gi]
            nkt = 4 * (I + 1)
            if kt == 0:
                opt[I] = ops_.next()
            op_ = opt[I]

            def pv(e, op_=op_, pt=pt, kt=kt, nkt=nkt):
                for j in range(2):
                    ins = e.matmul(op_[:, :], vh[:, kt + j, :], pt[:, j, :], start=(kt + j == 0), stop=(kt + j == nkt - 1))
                return ins
            P.op("pe", pv, [vh.r[0], vh.r[1], pt.r[0]], [op_.r[0]])
            if kt + 2 == nkt:
                rc = rcp.next()
                ao = aos.next()
                P.op("dve", lambda e, rc=rc, op_=op_: e.reciprocal(out=rc[64:128, :], in_=op_[64:128, :]), [], [rc.r[0], op_.r[0]])
                P.op("dve", lambda e, rc=rc, op_=op_, ao=ao: e.tensor_tensor(out=ao[:, :], in0=op_[0:64, :], in1=rc[64:128, :], op=ALU.mult),
                     [rc.r[0]], [ao.r[0], op_.r[0]])
                dma(P, "sp", attn_dst(h, I), ao[:, :], [ao.r[0]], [ores])

        LA = 2
        pend = [emit_qk(gi) for gi in range(min(LA, len(groups)))]
        for gi in range(len(groups)):
            I_, kt_ = groups[gi]
            if kt_ == 0 and I_ in ins_at and nxt_prep:
                nxt_prep[ins_at[I_]]()
            if gi + LA < len(groups):
                pend.append(emit_qk(gi + LA))
            emit_pv(gi, pend.pop(0))


def moba_consts():
    onehot = np.zeros((32, S), np.float32)
    for n in range(32):
        onehot[n, n * 256:(n + 1) * 256] = 1.0
    k = np.arange(128)[:, None]
    q = np.arange(512)[None, :]
    tri = np.stack([((j * 128 + k) <= q).astype(np.float32) for j in range(4)], axis=1)
    ident = np.eye(128, dtype=np.float32)
    qb = (np.arange(64) // 2)[:, None]
    n = np.arange(32)[None, :]
    valid = (n < qb).astype(np.float32)
    own = (n == qb).astype(np.float32)
    cmask = (valid - 1.0) * 1e30
    g = np.stack([cmask.reshape(-1), valid.reshape(-1), own.reshape(-1)], axis=0).astype(np.float32)
    gconst = np.ascontiguousarray(np.broadcast_to(g[None], (128, 3, 2048))).astype(np.float32)
    return onehot, np.ascontiguousarray(tri), ident, gconst


DIL_D = (1, 4, 16)
DIL_MOFF = (0, 5, 13)
DIL_NM = 33


def phase_b_dil(cx, qT, kT, vv, dmask_d, attn_dst, ires, ores):
    P = cx.P
    Ks = [cx.sb([128, S], BF16, nres=2) for _ in range(3)]
    Vs = [cx.sb([128, 64, 128], BF16, nres=2) for _ in range(3)]
    mk = cx.sb([128, DIL_NM, 512], BF16, nres=DIL_NM)
    qts = Rot([cx.sb([128, 512], BF16, nres=2) for _ in range(6)])
    pts = Rot([cx.sb([128, 2, 512], BF16) for _ in range(4)])
    den = cx.sb([128, 512], F32)
    aos = Rot([cx.sb([64, 512], BF16) for _ in range(3)])
    sps = Rot([cx.ps([128, 1024]) for _ in range(2)])
    opg = [cx.ps([128, 512]) for _ in range(3)]
    cnt = 0
    for m in range(DIL_NM):
        dma(P, "sp", mk[:, m, :], dmask_d.ap()[:, m, :], [], [mk.r[m]])
    vview = vv.ap().rearrange("(kt p) n -> p kt n", p=128)
    cnt = 0
    for slot in range(2):
        if slot == 1:
            pass
        for g in range(3):
            hl = g * 2 + slot
            for q4 in range(4):
                sl = slice(q4 * 2048, (q4 + 1) * 2048)
                dma(P, "pool", Ks[g][0:64, sl], kT.ap()[hl * 64:(hl + 1) * 64, sl], [ires], [Ks[g].r[0]])
                dma(P, "pool", Vs[g][:, q4 * 16:(q4 + 1) * 16, 0:64], vview[:, q4 * 16:(q4 + 1) * 16, hl * 64:(hl + 1) * 64], [ires], [Vs[g].r[0]])
        if slot == 0:
            for b_ in Vs:
                P.op("pool", lambda e, b_=b_: e.memset(b_[:, :, 64:128], 1.0), [], [b_.r[1]])
            for b_ in Ks:
                P.op("pool", lambda e, b_=b_: e.memset(b_[64:128, :], 0.0), [], [b_.r[1]])
            for b_ in qts.tiles:
                P.op("pool", lambda e, b_=b_: e.memset(b_[64:128, :], 0.0), [], [b_.r[1]])

        def load_q(I, slot=slot):
            res = []
            for g in range(3):
                hl = g * 2 + slot
                qt = qts.next()
                dma(P, "pool", qt[0:64, :], qT.ap()[hl * 64:(hl + 1) * 64, I * 512:(I + 1) * 512], [ires], [qt.r[0]])
                res.append(qt)
            return res
        nxt_q = load_q(0)
        for I in range(NT):
            work = []
            cur_q = nxt_q
            if I + 1 < NT:
                nxt_q = load_q(I + 1)
            for g in range(3):
                hl = g * 2 + slot
                d = DIL_D[g]
                qt = cur_q[g]
                kts = [kt for kt in range(4 * I - d, 4 * I + 4) if kt >= 0]
                for c0 in range(0, len(kts), 2):
                    work.append((g, d, qt, kts, kts[c0:c0 + 2]))

            def emit_qk(w, I=I):
                g, d, qt, kts, grp = w
                n = len(grp)
                sp_ = sps.next()

                def smm(e, sp_=sp_, grp=grp, qt=qt, g=g):
                    for j, kt in enumerate(grp):
                        ins = e.matmul(sp_[:, j * 512:(j + 1) * 512], Ks[g][:, kt * 128:(kt + 1) * 128], qt[:, :], start=True, stop=True)
                    return ins
                P.op("pe", smm, [Ks[g].r[0], Ks[g].r[1], qt.r[0], qt.r[1]], [sp_.r[0]])
                pt = pts.next()
                P.op("act", lambda e, pt=pt, sp_=sp_, n=n: e.activation(out=pt[:, 0:n, :].rearrange("p a b -> p (a b)"),
                                                                        in_=sp_[:, 0:n * 512], func=AF.Exp), [], [pt.r[0], sp_.r[0]])
                m0 = DIL_MOFF[g] + (grp[0] - (4 * I - d))
                P.op("dve", lambda e, pt=pt, m0=m0, n=n: e.tensor_tensor(out=pt[:, 0:n, :], in0=pt[:, 0:n, :], in1=mk[:, m0:m0 + n, :], op=ALU.mult),
                     [mk.r[m0 + j] for j in range(n)], [pt.r[0]])
                return pt

            def emit_pv(w, pt):
                g, d, qt, kts, grp = w
                op_ = opg[g]

                def pv(e, op_=op_, pt=pt, grp=grp, kts=kts, g=g):
                    for j, kt in enumerate(grp):
                        ins = e.matmul(op_[:, :], Vs[g][:, kt, :], pt[:, j, :], start=(kt == kts[0]), stop=(kt == kts[-1]))
                    return ins
                P.op("pe", pv, [Vs[g].r[0], Vs[g].r[1], pt.r[0]], [op_.r[0]])

            LA = 2
            pend = [emit_qk(work[wi]) for wi in range(min(LA, len(work)))]
            for wi in range(len(work)):
                if wi + LA < len(work):
                    pend.append(emit_qk(work[wi + LA]))
                emit_pv(work[wi], pend.pop(0))
            P.op("act", lambda e: e.activation(out=den[64:128, :], in_=opg[0][64:128, :], func=AF.Copy), [], [den.r[0], opg[0].r[0]])
            P.op("dve", lambda e: e.tensor_tensor(out=den[64:128, :], in0=den[64:128, :], in1=opg[1][64:128, :], op=ALU.add), [], [den.r[0], opg[1].r[0]])
            P.op("dve", lambda e: e.tensor_tensor(out=den[64:128, :], in0=den[64:128, :], in1=opg[2][64:128, :], op=ALU.add), [], [den.r[0], opg[2].r[0]])
            P.op("dve", lambda e: e.reciprocal(out=den[64:128, :], in_=den[64:128, :]), [], [den.r[0]])
            for g in range(3):
                hl = g * 2 + slot
                ao = aos.next()
                P.op("dve", lambda e, ao=ao, g=g: e.tensor_tensor(out=ao[:, :], in0=opg[g][0:64, :], in1=den[64:128, :], op=ALU.mult),
                     [den.r[0]], [ao.r[0], opg[g].r[0]])
                dma(P, "sp", attn_dst(hl, I), ao[:, :], [ao.r[0]], [ores])


def dil_consts():
    k = np.arange(128)[:, None]
    q = np.arange(512)[None, :]
    tiles = []
    for g, d in enumerate(DIL_D):
        for delta in range(d + 4):
            diff = q - k + (d - delta) * 128
            tiles.append(((diff >= 0) & (diff <= 128 * d) & (diff % d == 0)).astype(np.float32))
    return np.ascontiguousarray(np.stack(tiles, axis=1))


RG = [[0, 1], [2, 3], [4, 5], [6, 7]]


def all_gather(nc, srcs, dsts):
    sems = []
    for _ in srcs:
        _UID[0] += 1
        sems.append(nc.alloc_semaphore("cc%d" % _UID[0]))
    with nc.Block() as block:
        @block.gpsimd
        def _(g):
            for src, dst, sem in zip(srcs, dsts, sems):
                g.collective_compute("AllGather", ALU.bypass, replica_groups=RG, ins=[src.ap()], outs=[dst.ap()]).then_inc(sem)
            for sem in sems:
                g.wait_ge(sem, 1)


def build_program():
    nc = bass.Bass("TRN2", target_bir_lowering=False)
    ei = lambda name, shape, dt=F32: nc.dram_tensor(name, list(shape), dt, kind="ExternalInput")
    xT = ei("xT", [D, S])
    xres = ei("xres", [D, HALF])
    wqkv = [ei("wqkv0", [D, 3 * 512]), ei("wqkv1", [D, 3 * 384])]
    wo = [ei("wo0", [1024, D]), ei("wo1", [768, D])]
    win = [ei("win0", [D, DFF]), ei("win1", [D, DFF])]
    wout = [ei("wout0", [DFF, D]), ei("wout1", [DFF, D])]
    lnp = [ei("lnp0", [128, 32]), ei("lnp1", [128, 32])]
    cos_d, sin_d, rot_d = ei("cos", [128, S]), ei("sin", [128, S]), ei("rot", [128, 128])
    onehot_d, tri_d, ident_d = ei("onehot", [32, S], BF16), ei("tri", [128, 4, 512], BF16), ei("ident", [128, 128], BF16)
    gconst_d, dmask_d = ei("gconst", [128, 3, 2048]), ei("dmask", [128, DIL_NM, 512], BF16)
    hsel_d = ei("hsel", [128, 2])
    yT = nc.dram_tensor("yT", [D, HALF], F32, kind="ExternalOutput")
    it = lambda name, shape, dt=BF16: nc.dram_tensor(name, list(shape), dt)
    nhs = (8, 6)
    qT = [it("qT%d" % l, [nhs[l] * 64, S]) for l in range(2)]
    kT = [it("kT%d" % l, [nhs[l] * 64, S]) for l in range(2)]
    vv = [it("v%d" % l, [S, nhs[l] * 64]) for l in range(2)]
    CH = 256
    a_in = [[it("ain%d_%d" % (l, j), [CH, HALF]) for j in range(2 * nhs[l] * 64 // CH)] for l in range(2)]
    a_g = [[it("ag%d_%d" % (l, j), [2 * CH, HALF]) for j in range(2 * nhs[l] * 64 // CH)] for l in range(2)]
    wp = [it("wp%d" % l, [9, 128, 8192]) for l in range(2)]
    res1 = it("res1", [D, HALF], F32)
    hg_in = [it("hgin%d" % j, [CH, HALF]) for j in range(D // CH)]
    hg_out = [it("hgout%d" % j, [2 * CH, HALF]) for j in range(D // CH)]

    xv = xT.ap().rearrange("(c p) s -> p c s", p=128)
    xrv = xres.ap().rearrange("(c p) s -> p c s", p=128)
    r1v = res1.ap().rearrange("(c p) s -> p c s", p=128)
    hgi = [h_.ap().rearrange("(c p) s -> p c s", p=128) for h_ in hg_in]
    hgo = [h_.ap().rearrange("(r c p) s -> r p c s", r=2, p=128) for h_ in hg_out]
    yv = yT.ap().rearrange("(c p) s -> p c s", p=128)
    tile_sl = lambda v: (lambda tt: v[:, :, tt * 512:(tt + 1) * 512])

    import os
    pool = SemPool(nc)
    nstage = int(os.environ.get("DBG_STAGES", "99"))
    stage = [0]

    def go():
        stage[0] += 1
        return stage[0] <= nstage
    for l in range(2):
        nh = nhs[l]
        nk = 2 * nh * 64 // 128
        if not go():
            break
        cx = Ctx(nc)
        with cx.st:
            hook = None
            if l == 0:
                hsrc, f32 = (lambda t: [(0, 8, xv[:, :, t * 512:(t + 1) * 512])]), True
            else:
                hsrc, f32 = (lambda t: [(2 * j, 2 * j + 2, hgo[j][t // 8][:, :, (t % 8) * 512:(t % 8 + 1) * 512]) for j in range(4)]), False
            phase_a(cx, hsrc, f32, wqkv[l], cos_d, sin_d, rot_d, qT[l], kT[l], vv[l], nh, Res(), Res(), hook=hook)
            cx.P.emit(pool)
        if not go():
            break
        cx = Ctx(nc)
        with cx.st:
            nhd = nh * 64
            def dst(h, I, l=l, nhd=nhd):
                row0 = (I // 8) * nhd + h * 64
                return a_in[l][row0 // CH].ap()[row0 % CH:row0 % CH + 64, (I % 8) * 512:(I % 8 + 1) * 512]
            if l == 0:
                stg = Rot([cx.sb([128, 8192], BF16, nres=4) for _ in range(2)])
                pc = [precast_weights(cx, wo[ll], win[ll], wout[ll], wp[ll], Res(), nk=2 * nhs[ll] * 64 // 128, stg=stg) for ll in range(2)]
                todo = [(ll, i) for ll in range(2) for i in range(9)]

                def mhook(h, pc=pc, todo=todo):
                    for (ll, i) in todo[h * 3:(h + 1) * 3]:
                        pc[ll](i)
                phase_b_moba(cx, qT[l], kT[l], vv[l], onehot_d, tri_d, ident_d, gconst_d, dst, nh, Res(), Res(), hook=mhook)
            else:
                phase_b_dil(cx, qT[l], kT[l], vv[l], dmask_d, dst, Res(), Res())
            cx.P.emit(pool)
        all_gather(nc, a_in[l], a_g[l])
        if not go():
            break
        cx = Ctx(nc)
        with cx.st:
            nhd = nh * 64
            kh = nk // 2

            def attn_src(tt, l=l, nhd=nhd, kh=kh, nk=nk):
                res = []
                for hh in range(2):
                    lst = []
                    for k in range(nk):
                        row0 = hh * nhd + (k % kh) * 128
                        rr = (k // kh) * CH + row0 % CH
                        lst.append(a_g[l][row0 // CH].ap()[rr:rr + 128, tt * 512:(tt + 1) * 512])
                    res.append(lst)
                return res
            if l == 0:
                phase_c(cx, attn_src, tile_sl(xrv), wp[l], lnp[l], tile_sl(r1v), (lambda tt: [(2 * j, 2 * j + 2, hgi[j][:, :, tt * 512:(tt + 1) * 512]) for j in range(4)]), Res(), Res(), Res(), Res(), nk=nk, hsel_d=hsel_d)
            else:
                phase_c(cx, attn_src, tile_sl(r1v), wp[l], lnp[l], tile_sl(yv), None, Res(), Res(), Res(), Res(), nk=nk, hsel_d=hsel_d)
            cx.P.emit(pool)
        if l == 0:
            all_gather(nc, hg_in, hg_out)
    return nc


def _lnp(g1, b1, g2, b2):
    return np.ascontiguousarray(np.concatenate([np.asarray(v, np.float32).reshape(8, 128).T for v in (g1, b1, g2, b2)], axis=1))


def make_in_maps(x, moba_w_qkv, moba_w_o, dil_w_qkv, dil_w_o, mlp_w_in, mlp_w_out, ln_mix_g, ln_mix_b, ln_mlp_g, ln_mlp_b):
    f = lambda a: np.ascontiguousarray(np.asarray(a, dtype=np.float32))
    cos128, sin128, R = rope_tables()
    onehot, tri, ident, gconst = moba_consts()
    dmask = dil_consts()
    wq0 = f(moba_w_qkv)[0]
    wq1 = f(dil_w_qkv)[0]
    shared = {"win0": f(mlp_w_in)[0], "win1": f(mlp_w_in)[1], "wout0": f(mlp_w_out)[0], "wout1": f(mlp_w_out)[1],
              "lnp0": _lnp(ln_mix_g[0], ln_mix_b[0], ln_mlp_g[0], ln_mlp_b[0]),
              "lnp1": _lnp(ln_mix_g[1], ln_mix_b[1], ln_mlp_g[1], ln_mlp_b[1]),
              "cos": cos128, "sin": sin128, "rot": R, "onehot": onehot.astype(ml_dtypes.bfloat16), "tri": tri.astype(ml_dtypes.bfloat16),
              "ident": ident.astype(ml_dtypes.bfloat16), "gconst": gconst, "dmask": dmask.astype(ml_dtypes.bfloat16),
              "wo0": f(moba_w_o)[0]}
    heads1 = [[g * 4 + 2 * r + sl for g in range(3) for sl in range(2)] for r in range(2)]
    rows1 = np.concatenate([np.arange(h * 64, (h + 1) * 64) for r in range(2) for h in heads1[r]])
    shared["wo1"] = np.ascontiguousarray(f(dil_w_o)[0][rows1])
    maps = []
    xf = f(x)
    for c in range(8):
        b, r = c // 2, c % 2
        xT = np.ascontiguousarray(xf[b].T)
        m = dict(shared)
        m["xT"] = xT
        hs = np.zeros((128, 2), np.float32)
        hs[:, r] = 1.0
        m["hsel"] = hs
        m["xres"] = np.ascontiguousarray(xT[:, r * HALF:(r + 1) * HALF])
        cols0 = np.concatenate([np.arange(part * 1024 + (8 * r) * 64, part * 1024 + (8 * r + 8) * 64) for part in range(3)])
        m["wqkv0"] = np.ascontiguousarray(wq0[:, cols0])
        cols1 = np.concatenate([np.arange(part * 768 + h * 64, part * 768 + (h + 1) * 64) for part in range(3) for h in heads1[r]])
        m["wqkv1"] = np.ascontiguousarray(wq1[:, cols1])
        maps.append(m)
    return maps


def kernel(**inputs):
    nc = build_program()
    maps = make_in_maps(**inputs)
    res = run_bass_kernel_spmd(nc, maps, core_ids=list(range(8)))
    out = np.empty((B, S, D), np.float32)
    for c in range(8):
        b, r = c // 2, c % 2
        out[b, r * HALF:(r + 1) * HALF, :] = np.asarray(res.results[c]["yT"]).T
    return out
```
